# Optimizing a Trainium2 kernel written in Bass

```python
import jax, jax.numpy as jnp
from jax import lax
import numpy as np

D_MODEL = 1024
BATCH = 2
SEQ = 8192
DEPTH = 2

CHUNK = 64
N_MIXERS = 2
CONV_WIDTH = 3
N_HEADS = 16
HEAD_DIM = D_MODEL // N_HEADS
LEFT_CHUNKS = 8
BAND = (LEFT_CHUNKS + 1) * CHUNK
MAX_REL_DIST = 256
N_REL = 2 * MAX_REL_DIST + 1
D_FF = (((8 * D_MODEL + 2) // 3 + 255) // 256) * 256
N_CONV_LAYERS = (DEPTH + N_MIXERS - 1) // N_MIXERS
N_ATTN_LAYERS = DEPTH // N_MIXERS
N_ADA = 6
RMS_EPS = 1e-6
NEG_INF = -1e30

kernel_name = "hybrid_conv_chunkattn_sandwich_adaln"


def rms_norm(x, g):
    xf = x.astype(jnp.float32)
    y = xf * lax.rsqrt(jnp.mean(xf * xf, axis=-1, keepdims=True) + RMS_EPS)
    return (y * g.astype(jnp.float32)).astype(x.dtype)


def modulate(h, shift, scale):
    return h * (1.0 + scale[:, None, :]) + shift[:, None, :]


def short_conv_mixer(h, w_in, w_conv, w_out):
    d = h.shape[-1]
    bcv = h @ w_in
    gate_b, gate_c, v = jnp.split(bcv, 3, axis=-1)
    u = gate_c * v
    conv = lax.conv_general_dilated(
        u, w_conv[:, None, :].astype(u.dtype),
        window_strides=(1,), padding=[(CONV_WIDTH - 1, 0)],
        dimension_numbers=("NWC", "WIO", "NWC"), feature_group_count=d)
    return (gate_b * conv) @ w_out


def _rel_bias_index():
    qi = np.arange(CHUNK)[:, None]
    kj = np.arange(BAND)[None, :]
    dist = qi + LEFT_CHUNKS * CHUNK - kj
    return (np.clip(dist, -MAX_REL_DIST, MAX_REL_DIST) + MAX_REL_DIST).astype(np.int32)


def chunked_rel_attention(h, w_qkv, rel_bias, w_out):
    b, s, d = h.shape
    n_chunks = s // CHUNK
    qkv = (h @ w_qkv).reshape(b, s, 3, N_HEADS, HEAD_DIM)
    q = qkv[:, :, 0] * (HEAD_DIM ** -0.5)
    k = qkv[:, :, 1]
    v = qkv[:, :, 2]
    pad = ((0, 0), (LEFT_CHUNKS * CHUNK, 0), (0, 0), (0, 0))
    k_pad = jnp.pad(k, pad)
    v_pad = jnp.pad(v, pad)
    q_chunks = q.reshape(b, n_chunks, CHUNK, N_HEADS, HEAD_DIM).transpose(1, 0, 2, 3, 4)
    bias = rel_bias.astype(jnp.float32)[:, jnp.asarray(_rel_bias_index())]
    key_slot = jnp.arange(BAND)

    def one_chunk(args):
        n, qc = args
        kb = lax.dynamic_slice_in_dim(k_pad, n * CHUNK, BAND, axis=1)
        vb = lax.dynamic_slice_in_dim(v_pad, n * CHUNK, BAND, axis=1)
        scores = jnp.einsum("bqhd,bkhd->bhqk", qc, kb).astype(jnp.float32) + bias[None]
        valid = key_slot >= (LEFT_CHUNKS - n) * CHUNK
        scores = jnp.where(valid[None, None, None, :], scores, NEG_INF)
        p = jax.nn.softmax(scores, axis=-1).astype(vb.dtype)
        return jnp.einsum("bhqk,bkhd->bqhd", p, vb)

    out = lax.map(one_chunk, (jnp.arange(n_chunks, dtype=jnp.int32), q_chunks))
    out = out.transpose(1, 0, 2, 3, 4).reshape(b, s, d)
    return out @ w_out


def swiglu(h, w_gate_up, w_down):
    g, u = jnp.split(h @ w_gate_up, 2, axis=-1)
    return (jax.nn.silu(g) * u) @ w_down


def setup_inputs(seed: int = 0) -> dict:
    key = jax.random.key(seed)
    ks = jax.random.split(key, 14)
    f32 = jnp.float32
    sd = D_MODEL ** -0.5
    x = jax.random.normal(ks[0], (BATCH, SEQ, D_MODEL), f32)
    c = jax.random.normal(ks[1], (BATCH, D_MODEL), f32)
    ada_w = jax.random.normal(ks[2], (DEPTH, D_MODEL, N_ADA * D_MODEL), f32) * (0.5 * sd)
    ada_b = jax.random.normal(ks[3], (DEPTH, N_ADA * D_MODEL), f32) * 0.02
    norm_gains = 1.0 + 0.02 * jax.random.normal(ks[4], (DEPTH, 4, D_MODEL), f32)
    conv_w_in = jax.random.normal(ks[5], (N_CONV_LAYERS, D_MODEL, 3 * D_MODEL), f32) * sd
    conv_w = jax.random.normal(ks[6], (N_CONV_LAYERS, CONV_WIDTH, D_MODEL), f32) * (CONV_WIDTH ** -0.5)
    conv_w_out = jax.random.normal(ks[7], (N_CONV_LAYERS, D_MODEL, D_MODEL), f32) * sd
    attn_w_qkv = jax.random.normal(ks[8], (N_ATTN_LAYERS, D_MODEL, 3 * D_MODEL), f32) * sd
    attn_rel_bias = jax.random.normal(ks[9], (N_ATTN_LAYERS, N_HEADS, N_REL), f32) * 0.1
    attn_w_out = jax.random.normal(ks[10], (N_ATTN_LAYERS, D_MODEL, D_MODEL), f32) * sd
    ffn_w_gate_up = jax.random.normal(ks[11], (DEPTH, D_MODEL, 2 * D_FF), f32) * sd
    ffn_w_down = jax.random.normal(ks[12], (DEPTH, D_FF, D_MODEL), f32) * (D_FF ** -0.5)
    return {"x": x, "c": c, "ada_w": ada_w, "ada_b": ada_b, "norm_gains": norm_gains,
            "conv_w_in": conv_w_in, "conv_w": conv_w, "conv_w_out": conv_w_out,
            "attn_w_qkv": attn_w_qkv, "attn_rel_bias": attn_rel_bias, "attn_w_out": attn_w_out,
            "ffn_w_gate_up": ffn_w_gate_up, "ffn_w_down": ffn_w_down}


def reference(x, c, ada_w, ada_b, norm_gains, conv_w_in, conv_w, conv_w_out,
              attn_w_qkv, attn_rel_bias, attn_w_out, ffn_w_gate_up, ffn_w_down):
    c_act = jax.nn.silu(c)
    for i in range(DEPTH):
        mod = c_act @ ada_w[i] + ada_b[i]
        sh_m, sc_m, g_m, sh_f, sc_f, g_f = jnp.split(mod, N_ADA, axis=-1)
        h = modulate(rms_norm(x, norm_gains[i, 0]), sh_m, sc_m)
        j = i // N_MIXERS
        if i % N_MIXERS == 0:
            h = short_conv_mixer(h, conv_w_in[j], conv_w[j], conv_w_out[j])
        else:
            h = chunked_rel_attention(h, attn_w_qkv[j], attn_rel_bias[j], attn_w_out[j])
        x = x + g_m[:, None, :] * rms_norm(h, norm_gains[i, 1])
        h = modulate(rms_norm(x, norm_gains[i, 2]), sh_f, sc_f)
        h = swiglu(h, ffn_w_gate_up[i], ffn_w_down[i])
        x = x + g_f[:, None, :] * rms_norm(h, norm_gains[i, 3])
    return x
```

```python
from contextlib import ExitStack

import ml_dtypes
import numpy as np

import concourse.bass as bass
import concourse.mybir as mybir
from concourse.bass_utils import run_bass_kernel_spmd

F32 = mybir.dt.float32
BF16 = mybir.dt.bfloat16
AF = mybir.ActivationFunctionType
ALU = mybir.AluOpType
AX = mybir.AxisListType

D = 1024
KC = 8
NT = 512
NB = 4
DFF = 2816
JC = 22
NH = 16
TOK_CORE = 2048
HALO = 512
EPS = 1e-6
NEG = -30000.0
NSLOT = 5
N_SCRATCH = 52

ENGS = ["pe", "act", "dve", "pool", "sp"]
EIDX = {e: i for i, e in enumerate(ENGS)}


class Op:
    __slots__ = ("eng", "fn", "reads", "writes", "dma_key", "deps", "idx", "signal", "signo", "vc", "dvc",
                 "dma_val", "waits", "label")

    def __init__(self, eng, fn, reads, writes, dma_key):
        self.eng = eng
        self.fn = fn
        self.reads = reads
        self.writes = writes
        self.dma_key = dma_key
        self.deps = []
        self.idx = 0
        self.signal = False
        self.signo = 0
        self.vc = None
        self.dvc = None
        self.dma_val = 0
        self.waits = []


class Graph:
    def __init__(self, nc):
        self.nc = nc
        self.ops = []
        self.last_writer = {}
        self.readers = {}
        self.dma_count = {}
        self.phase = ""

    def op(self, eng, fn, reads=(), writes=(), dma_key=None):
        o = Op(eng, fn, tuple(reads), tuple(writes), dma_key)
        o.label = self.phase
        deps = {}
        for k in o.reads:
            w = self.last_writer.get(k)
            if w is not None:
                deps[id(w)] = (w, "raw")
            if type(k) is tuple and k[0] == "ps":
                for r in self.readers.get(k, ()):
                    if r.eng != eng and id(r) not in deps:
                        deps[id(r)] = (r, "rar")
        for k in o.writes:
            w = self.last_writer.get(k)
            if w is not None and id(w) not in deps:
                deps[id(w)] = (w, "waw")
            for r in self.readers.get(k, ()):
                if id(r) not in deps:
                    deps[id(r)] = (r, "war")
        for d, kind in deps.values():
            if d is o:
                continue
            same = (d.eng == o.eng) and d.dma_key is None and o.dma_key is None
            if same and o.eng == "pe":
                continue
            o.deps.append(d)
        for k in o.writes:
            self.last_writer[k] = o
            self.readers[k] = []
        for k in o.reads:
            self.readers.setdefault(k, []).append(o)
        if dma_key is not None:
            self.dma_count[dma_key] = self.dma_count.get(dma_key, 0) + 1
            o.dma_val = 16 * self.dma_count[dma_key]
        self.ops.append(o)
        return o

    def finalize(self, ctx):
        nc = self.nc
        ne = len(ENGS)
        cnt = [0] * ne
        seen = [[0] * ne for _ in range(ne)]
        seen_dma = [dict() for _ in range(ne)]
        per_eng = {e: [] for e in ENGS}
        order = {id(o): i for i, o in enumerate(self.ops)}
        for o in self.ops:
            e = EIDX[o.eng]
            if o.dma_key is None:
                cnt[e] += 1
                o.idx = cnt[e]
            st = seen[e]
            sd = seen_dma[e]
            for d in sorted(o.deps, key=lambda d: -order[id(d)]):
                if d.dma_key is not None:
                    if sd.get(d.dma_key, 0) >= d.dma_val:
                        continue
                    o.waits.append(("dma", d.dma_key, d.dma_val))
                    sd[d.dma_key] = d.dma_val
                else:
                    p = EIDX[d.eng]
                    if st[p] >= d.idx:
                        continue
                    d.signal = True
                    o.waits.append(("eng", d))
                for i in range(ne):
                    if d.vc[i] > st[i]:
                        st[i] = d.vc[i]
                for k2, v2 in d.dvc.items():
                    if sd.get(k2, 0) < v2:
                        sd[k2] = v2
            o.vc = list(st)
            o.dvc = dict(sd)
            if o.dma_key is None:
                o.vc[e] = o.idx
            per_eng[o.eng].append(o)
        signo = [0] * ne
        for o in self.ops:
            if o.dma_key is None and o.signal:
                e = EIDX[o.eng]
                signo[e] += 1
                o.signo = signo[e]
        esem = {e: ctx.enter_context(nc.semaphore("s_" + e)) for e in ENGS}
        dsem = {}
        for i, k in enumerate(self.dma_count):
            dsem[k] = ctx.enter_context(nc.semaphore("d%d" % i))
        self.n_sems = len(esem) + len(dsem)

        def emit_stream(engname, h):
            for o in per_eng[engname]:
                for w in o.waits:
                    if w[0] == "dma":
                        h.wait_ge(dsem[w[1]], w[2])
                    else:
                        h.wait_ge(esem[w[1].eng], w[1].signo)
                inst = o.fn(h)
                if inst is None:
                    continue
                if o.dma_key is not None:
                    inst.then_inc(dsem[o.dma_key], 16)
                elif o.signal:
                    inst.then_inc(esem[engname], 1)

        with nc.Block() as block:
            @block.tensor
            def _(eng):
                emit_stream("pe", nc.tensor)

            @block.scalar
            def _(eng):
                emit_stream("act", nc.scalar)

            @block.vector
            def _(eng):
                emit_stream("dve", nc.vector)

            @block.gpsimd
            def _(eng):
                emit_stream("pool", nc.gpsimd)

            @block.sync
            def _(eng):
                emit_stream("sp", nc.sync)


class Builder:
    def __init__(self, mode="fused", n_tiles=None):
        self.mode = mode
        self.do_l0 = mode in ("fused", "l0")
        self.do_l1 = mode in ("fused", "l1")
        if mode == "l0":
            self.n_tiles = 4 if n_tiles is None else n_tiles
            self.n_in_rows = 2 + self.n_tiles * NT
            self.first_out_tile = 0
        else:
            self.n_tiles = 5 if n_tiles is None else n_tiles
            self.n_in_rows = (2 if mode == "fused" else 0) + self.n_tiles * NT
            self.first_out_tile = 1
        self.n_out_rows = (self.n_tiles - self.first_out_tile) * NT
        self.layers = ([0] if self.do_l0 else []) + ([1] if self.do_l1 else [])
        self.nc = bass.Bass("TRN2", target_bir_lowering=False)
        self.ctx = ExitStack()
        self.bank_rr = 0
        self.tbank_rr = 0
        self.slot_rr = 0
        self.unit_idx = {}
        self.unit_uses = {}
        self.tile_t = 0
        self.mod_done = set()

    def dram_in(self, name, shape, dt=F32):
        return self.nc.dram_tensor(name, list(shape), dt, kind="ExternalInput").ap()

    def sb(self, name, shape, dt):
        return self.ctx.enter_context(self.nc.sbuf_tensor(name, list(shape), dt))

    def declare(self):
        nc = self.nc
        self.xa = self.dram_in("xa", [self.n_in_rows, D])
        self.cvec = self.dram_in("cvec", [128, KC])
        self.ada_w = self.dram_in("ada_w", [2, D, 6 * D])
        self.ada_b = self.dram_in("ada_b", [2, 6 * D])
        self.ada_bT = self.dram_in("ada_bT", [2, 128, 48])
        self.gains = self.dram_in("gains", [2, 4, D])
        self.gainsT = self.dram_in("gainsT", [2, 4, 128, KC])
        self.conv_w_in = self.dram_in("conv_w_in", [D, 3 * D])
        self.conv_wT = self.dram_in("conv_wT", [128, KC, 3])
        self.conv_w_out = self.dram_in("conv_w_out", [D, D])
        self.attn_w_qkv = self.dram_in("attn_w_qkv", [D, 3 * D])
        self.attn_w_out = self.dram_in("attn_w_out", [D, D])
        self.ffn_gu = self.dram_in("ffn_gu", [2, D, 2 * DFF])
        self.ffn_down = self.dram_in("ffn_down", [2, DFF, D])
        self.bias_t = self.dram_in("bias_t", [128, NH, 640])
        self.hmask = self.dram_in("hmask", [2, 512], BF16)
        self.cmask = self.dram_in("cmask", [128, 1])
        self.idbf = self.dram_in("idbf", [128, 128], BF16)
        self.out = nc.dram_tensor("out", [self.n_out_rows, D], F32, kind="ExternalOutput").ap()
        self.scratch = nc.dram_tensor("wscratch", [N_SCRATCH, 128, KC * 512], BF16, kind="Internal").ap()

        sb = self.sb
        self.x_sb = sb("x_sb", [128, NB + 1, D], F32)
        self.xn = sb("xn", [128, 2, D], BF16)
        self.junk = sb("junk", [128, D], BF16)
        self.hT = sb("hT", [128, KC, NT], BF16)
        self.hpre = sb("hpre", [128, KC, 2], BF16)
        self.act = sb("act", [128, JC, NT], BF16)
        self.tres = sb("tres", [128, D], F32)
        self.arena = sb("arena", [128, 5, 640], F32)
        self.G = sb("G", [128, 4, D], F32)
        self.AB = sb("AB", [128, 2, 4, KC], F32)
        self.modfm = sb("modfm", [128, 4, KC], F32)
        self.gT = sb("gT", [128, 2, KC], F32)
        self.bT = sb("bT", [128, 48], F32)
        self.stats = sb("stats", [128, 96], F32)
        self.ident = sb("ident", [128, 128], BF16)
        self.c_sb = sb("c_sb", [128, KC], F32)
        self.c_bf2 = sb("c_bf2", [128, KC, 2], BF16)
        self.cT_rep = sb("cT_rep", [128, KC, 128], BF16)
        self.cmask_sb = sb("cmask_sb", [128, 1], F32)
        self.wring = sb("wring", [128, NSLOT, KC, 512], BF16)
        if self.do_l0:
            self.cw = sb("cw", [128, KC, 3], F32)
            self.carry = sb("carry", [128, KC, 2], F32)
            self.vpre = sb("vpre", [128, 2], F32)
        if self.do_l1:
            self.KT = sb("KT", [128, KC, 2 * NT], BF16)
            self.V = sb("V", [128, 2 * NB, D], BF16)
            self.bias = sb("bias", [128, NH, 640], F32)
            self.attn_sb = sb("attn_sb", [128, D], BF16)
            self.ones2 = sb("ones2", [128, 128], BF16)
            self.mrow = sb("mrow", [128, 512], BF16)
        self.ps = self.ctx.enter_context(nc.psum_tensor("ps", [128, 8, 512], F32))
        self.g = Graph(nc)

    def psb(self, bank):
        return self.ps[:, bank, :].bitcast(BF16)

    def arb(self, i):
        return self.arena[:, i, :].bitcast(BF16)

    def tsbuf(self, i):
        if i < 2:
            return self.arena[:, i, :], self.ar(i)
        return self.tres[:, 0:640], [("tres", 0), ("tres", 1)]

    def pbuf(self, i):
        c, hf = self.pkey(i)[1:]
        return self.arb(c)[:, hf * 640:(hf + 1) * 640]

    @staticmethod
    def pkey(i):
        return ("ar", 2 + i // 2, i % 2)

    def ptbuf(self, i):
        c, hf = self.ptkey(i)[1:]
        return self.arb(c)[:, hf * 640:(hf + 1) * 640]

    @staticmethod
    def ptkey(i):
        j = i + 1
        return ("ar", 3 + j // 2, j % 2)

    @staticmethod
    def ar(i):
        return [("ar", i, 0), ("ar", i, 1)]

    def xs(self, b, t=None):
        return (b - (self.tile_t if t is None else t)) % (NB + 1)

    def tbank(self):
        b = self.tbank_rr
        self.tbank_rr ^= 1
        return b

    def mbank(self):
        b = 2 + self.bank_rr
        self.bank_rr = (self.bank_rr + 1) % 6
        return b

    def mpair(self):
        if self.bank_rr % 2:
            self.bank_rr = (self.bank_rr + 1) % 6
        b = 2 + self.bank_rr
        self.bank_rr = (self.bank_rr + 2) % 6
        return b

    def st(self, c0, c1=None):
        return self.stats[:, c0:(c0 + 1 if c1 is None else c1)]

    def get_unit(self, name, parts, nk=KC):
        g = self.g
        s = self.slot_rr
        self.slot_rr = (self.slot_rr + 1) % NSLOT
        slot = self.wring[:, s]
        keys = [("w", s, p) for p in range(3)]
        skey = ("wslot", s)
        width = max(c0 + n for c0, n, _ in parts)
        ui, scr = None, None
        if name[0] != "ada":
            if name not in self.unit_idx:
                self.unit_idx[name] = len(self.unit_idx)
            ui = self.unit_idx[name]
            assert ui < N_SCRATCH, ui
            scr = self.scratch[ui].rearrange("p (k c) -> p k c", k=KC)[:, 0:nk, 0:width]
        uses = self.unit_uses.get(name, 0) + 1
        self.unit_uses[name] = uses
        if name[0] == "ada" or uses <= 2:
            used = []
            for pi, (c0, n, src) in enumerate(parts):
                dst = slot[:, 0:nk, c0:c0 + n]
                g.op("pool", lambda e, dst=dst, src=src: e.dma_start(out=dst, in_=src),
                     writes=([skey] if pi == 0 else []) + [keys[pi]], dma_key=("wc", s, pi))
                used.append(keys[pi])
            if name[0] != "ada" and uses == 2:
                g.op("sp", lambda e, scr=scr, s=s, nk=nk, width=width: e.dma_start(out=scr, in_=self.wring[:, s, 0:nk, 0:width]),
                     reads=[skey] + used, writes=[("scr", ui)], dma_key=("wb", s))
            return s, [skey] + used
        g.op("sp", lambda e, scr=scr, s=s, nk=nk, width=width: e.dma_start(out=self.wring[:, s, 0:nk, 0:width], in_=scr),
             reads=[("scr", ui)], writes=[skey] + keys, dma_key=("wl", s))
        return s, [skey] + keys

    def wsrc(self, w2d, c0, n, r0=0, nk=KC):
        return w2d.rearrange("(k p) n -> p k n", p=128)[:, r0:r0 + nk, c0:c0 + n]

    def prologue(self):
        g, nc = self.g, self.nc
        ld = lambda dst, src, key: g.op("sp", lambda e: e.dma_start(out=dst, in_=src), writes=[key], dma_key=("c", key))
        ld(self.ident[:], self.idbf, "ident")
        ld(self.c_sb[:], self.cvec, "c_sb")
        ld(self.cmask_sb[:], self.cmask, "cmask")
        if self.do_l0:
            ld(self.cw[:], self.conv_wT, "cw")
        if self.do_l1:
            ld(self.bias[:], self.bias_t, "bias")
            g.op("pool", lambda e: e.memset(self.mrow[:], 0.0), writes=["mrow0"])
            g.op("sp", lambda e: e.dma_start(out=self.mrow[0:1, :], in_=self.hmask[0:1, :]), reads=["mrow0"], writes=["mrow0"], dma_key=("c", "mrow0"))
            g.op("pool", lambda e: e.memset(self.ones2[:], 0.0), writes=["ones2"])
            g.op("pool", lambda e: e.memset(self.ones2[0:1, :], 1.0), reads=["ones2"], writes=["ones2"])
        g.op("pool", lambda e: e.memset(self.stats[:, 72:76], -0.5), writes=["mhalf"])
        g.op("act", lambda e: e.activation(out=self.c_sb[:], in_=self.c_sb[:], func=AF.Silu), reads=["c_sb"], writes=["c_act"])
        g.op("dve", lambda e: e.tensor_copy(out=self.c_bf2[:], in_=self.c_sb[:].unsqueeze(2).to_broadcast([128, KC, 2])),
             reads=["c_act"], writes=["c_bf2"])
        g.op("dve", lambda e: e.tensor_copy(out=self.cT_rep[:], in_=self.c_sb[:].unsqueeze(2).to_broadcast([128, KC, 128])),
             reads=["c_act"], writes=["cT_rep"])

    def ensure_mod(self, L, sub):
        if (L, sub) in self.mod_done:
            return
        self.mod_done.add((L, sub))
        g = self.g
        g.phase = "mod"
        adaw = self.ada_w[L]
        g.op("sp", lambda e: e.dma_start(out=self.bT[:], in_=self.ada_bT[L]), writes=["bT"], dma_key=("c", "bT"))
        for sub, (v_sh, v_sc, v_g, gi0, gi1) in ((sub, ((0, 1, 2, 0, 1), (3, 4, 5, 2, 3))[sub]),):
            pb = self.mbank()
            for vi, v in enumerate((v_sh, v_sc)):
                for half in range(2):
                    c0 = v * D + half * 512
                    s, keys = self.get_unit(("ada", L, v, half), [(0, 512, self.wsrc(adaw, c0, 512))])

                    def mm(e, s=s, vi=vi, half=half, pb=pb):
                        ins = None
                        for fcl in range(4):
                            col = (vi * KC + half * 4 + fcl) * 2
                            for kc in range(KC):
                                ins = e.matmul(self.ps[:, pb, col:col + 2], lhsT=self.wring[:, s, kc, fcl * 128:(fcl + 1) * 128],
                                               rhs=self.c_bf2[:, kc, :], start=(kc == 0), stop=(kc == KC - 1))
                        return ins
                    g.op("pe", mm, reads=keys + ["c_bf2"], writes=[("ps", pb)])
            gsrc = self.gainsT[L, gi0]
            g.op("sp", lambda e, gsrc=gsrc, sub=sub: e.dma_start(out=self.gT[:, sub, :], in_=gsrc), writes=[("gT", sub)], dma_key=("c", "gT", sub))
            psv = self.ps[:, pb, 0:32].rearrange("p (v k two) -> p v k two", v=2, two=2)[:, :, :, 0]

            def ev(e, psv=psv, v_sh=v_sh, sub=sub):
                bview = self.bT[:, v_sh * KC:(v_sh + 2) * KC].rearrange("p (v k) -> p v k", v=2)
                return e.tensor_tensor(out=self.modfm[:, 2 * sub:2 * sub + 2, :], in0=psv, in1=bview, op=ALU.add)
            g.op("dve", ev, reads=[("ps", pb), "bT"], writes=[("modfm", sub)])
            g.op("dve", lambda e, sub=sub, L=L: e.scalar_tensor_tensor(out=self.AB[:, L, 2 * sub, :], in0=self.modfm[:, 2 * sub + 1, :], scalar=1.0,
                                                                      in1=self.gT[:, sub, :], op0=ALU.add, op1=ALU.mult),
                 reads=[("modfm", sub), ("gT", sub)], writes=[("AB", L, 2 * sub)])
            g.op("dve", lambda e, sub=sub, L=L: e.tensor_copy(out=self.AB[:, L, 2 * sub + 1, :], in_=self.modfm[:, 2 * sub, :]),
                 reads=[("modfm", sub)], writes=[("AB", L, 2 * sub + 1)])
            for half in range(2):
                c0 = v_g * D + half * 512
                s, keys = self.get_unit(("ada", L, v_g, half), [(0, 512, self.wsrc(adaw, c0, 512))])
                pg = self.mbank()

                def mmg(e, s=s, pg=pg):
                    ins = None
                    for kc in range(KC):
                        ins = e.matmul(self.ps[:, pg, :], lhsT=self.cT_rep[:, kc, :], rhs=self.wring[:, s, kc, :],
                                       start=(kc == 0), stop=(kc == KC - 1))
                    return ins
                g.op("pe", mmg, reads=keys + ["cT_rep"], writes=[("ps", pg)])
                bsrc = self.ada_b[L, c0:c0 + 512].partition_broadcast(128)
                gsrc2 = self.gains[L, gi1, half * 512:(half + 1) * 512].partition_broadcast(128)
                g.op("sp", lambda e, bsrc=bsrc: e.dma_start(out=self.arena[:, 0, 0:512], in_=bsrc), writes=[*self.ar(0)], dma_key=("c", "ar0"))
                g.op("sp", lambda e, gsrc2=gsrc2: e.dma_start(out=self.arena[:, 1, 0:512], in_=gsrc2), writes=[*self.ar(1)], dma_key=("c", "ar1"))
                g.op("dve", lambda e, pg=pg: e.tensor_tensor(out=self.arena[:, 2, 0:512], in0=self.ps[:, pg, :], in1=self.arena[:, 0, 0:512], op=ALU.add),
                     reads=[("ps", pg), *self.ar(0)], writes=[*self.ar(2)])
                gidx = 2 * L + sub
                g.op("dve", lambda e, gidx=gidx, half=half: e.tensor_tensor(out=self.G[:, gidx, half * 512:(half + 1) * 512], in0=self.arena[:, 2, 0:512],
                                                                           in1=self.arena[:, 1, 0:512], op=ALU.mult),
                     reads=[*self.ar(2), *self.ar(1)], writes=[("G", gidx, half)])

    def rstd_chain(self, ss_c, ms_c, r_c, n, keys_in, key_out):
        g = self.g
        g.op("dve", lambda e: e.tensor_scalar(out=self.st(ms_c, ms_c + n), in0=self.st(ss_c, ss_c + n), scalar1=1.0 / D, scalar2=EPS,
                                              op0=ALU.mult, op1=ALU.add), reads=keys_in, writes=[("ms", ms_c)])
        g.op("pool", lambda e: e.tensor_tensor(out=self.st(r_c, r_c + n), in0=self.st(ms_c, ms_c + n), in1=self.st(72, 72 + n), op=ALU.pow),
             reads=[("ms", ms_c), "mhalf"], writes=[key_out])

    def prenorm(self, L, sub):
        g = self.g
        self.ensure_mod(L, sub)
        g.phase = self.tag + ".prenorm"
        for pair in range(NB // 2):
            blocks = (2 * pair, 2 * pair + 1)
            for b in blocks:
                sl = self.xs(b)
                g.op("act", lambda e, b=b, sl=sl: e.activation(out=self.junk[:], in_=self.x_sb[:, sl, :], func=AF.Square, accum_out=self.st(b)),
                     reads=[("x", sl)], writes=[("ss", b)] + [("junk", 0), ("junk", 1)])
            self.rstd_chain(2 * pair, 4 + 2 * pair, 8 + 2 * pair, 2, [("ss", b) for b in blocks], ("rstd", pair))
            for b in blocks:
                xb = b % 2
                sl = self.xs(b)
                g.op("act", lambda e, b=b, xb=xb, sl=sl: e.activation(out=self.xn[:, xb, :], in_=self.x_sb[:, sl, :], func=AF.Copy, scale=self.st(8 + b)),
                     reads=[("x", sl), ("rstd", pair)], writes=[("xn", xb)])
                tb = self.tbank()

                def tr(e, xb=xb, tb=tb):
                    ins = None
                    for kc in range(KC):
                        ins = e.transpose(out=self.psb(tb)[:, kc * 128:(kc + 1) * 128], in_=self.xn[:, xb, kc * 128:(kc + 1) * 128],
                                          identity=self.ident[:])
                    return ins
                g.op("pe", tr, reads=[("xn", xb), "ident"], writes=[("ps", tb)])
                for kc in range(KC):
                    def ev(e, kc=kc, b=b, tb=tb):
                        return e.tensor_scalar(out=self.hT[:, kc, b * 128:(b + 1) * 128], in0=self.psb(tb)[:, kc * 128:(kc + 1) * 128],
                                               scalar1=self.AB[:, L, 2 * sub, kc:kc + 1], scalar2=self.AB[:, L, 2 * sub + 1, kc:kc + 1],
                                               op0=ALU.mult, op1=ALU.add)
                    g.op("dve", ev, reads=[("ps", tb), ("AB", L, 2 * sub), ("AB", L, 2 * sub + 1)], writes=[("hT", kc, b)])

    def act_keys(self, js, blocks=range(NB)):
        return [("act", j, b) for j in js for b in blocks]

    def hT_keys(self, blocks=range(NB)):
        return [("hT", kc, b) for kc in range(KC) for b in blocks]

    def post_block(self, b, bank0, bank1, gidx):
        g = self.g
        g.op("act", lambda e: e.activation(out=self.junk[:, 0:512], in_=self.ps[:, bank0, :], func=AF.Square, accum_out=self.st(12 + b)),
             reads=[("ps", bank0)], writes=[("ssa", b), ("junk", 0)])
        g.op("act", lambda e: e.activation(out=self.junk[:, 512:1024], in_=self.ps[:, bank1, :], func=AF.Square, accum_out=self.st(16 + b)),
             reads=[("ps", bank1)], writes=[("ssb", b), ("junk", 1)])
        g.op("dve", lambda e: e.tensor_tensor(out=self.st(20 + b), in0=self.st(12 + b), in1=self.st(16 + b), op=ALU.add),
             reads=[("ssa", b), ("ssb", b)], writes=[("ss2", b)])
        g.op("dve", lambda e: e.tensor_scalar(out=self.st(24 + b), in0=self.st(20 + b), scalar1=1.0 / D, scalar2=EPS, op0=ALU.mult, op1=ALU.add),
             reads=[("ss2", b)], writes=[("ms2", b)])
        g.op("pool", lambda e: e.tensor_tensor(out=self.st(28 + b), in0=self.st(24 + b), in1=self.st(72), op=ALU.pow),
             reads=[("ms2", b), "mhalf"], writes=[("rstd2", b)])
        for n, bank in enumerate((bank0, bank1)):
            g.op("dve", lambda e, n=n, bank=bank: e.tensor_tensor(out=self.tres[:, n * 512:(n + 1) * 512], in0=self.ps[:, bank, :],
                                                                 in1=self.G[:, gidx, n * 512:(n + 1) * 512], op=ALU.mult),
                 reads=[("ps", bank), ("G", gidx, n)], writes=[("tres", n)])
        sl = self.xs(b)
        g.op("dve", lambda e: e.scalar_tensor_tensor(out=self.x_sb[:, sl, :], in0=self.tres[:], scalar=self.st(28 + b), in1=self.x_sb[:, sl, :],
                                                     op0=ALU.mult, op1=ALU.add),
             reads=[("tres", 0), ("tres", 1), ("rstd2", b), ("x", sl)], writes=[("x", sl)])

    def matmul2(self, specs, in_chunk, in_keys, gidx, on_done=None, nxt=None):
        g = self.g
        if nxt is not None:
            self.ensure_mod(*nxt)
        g.phase = self.tag + ".mm2a"
        held = [self.mbank() for _ in range(NB)]
        last = len(specs[0]) - 1
        for gi, (name, parts, j0, nj) in enumerate(specs[0]):
            s, keys = self.get_unit(name, parts, nk=nj)
            for b in range(NB):
                def mm(e, s=s, b=b, j0=j0, nj=nj, gi=gi):
                    ins = None
                    for jj in range(nj):
                        ins = e.matmul(self.ps[:, held[b], :], lhsT=in_chunk(j0 + jj, b), rhs=self.wring[:, s, jj, :],
                                       start=(gi == 0 and jj == 0), stop=(gi == last and jj == nj - 1))
                    return ins
                g.op("pe", mm, reads=keys + in_keys(b), writes=[("ps", held[b])])
        units = []
        for (name, parts, j0, nj) in specs[1]:
            s, keys = self.get_unit(name, parts, nk=nj)
            units.append((s, keys, j0, nj))
        tot = sum(u[3] for u in units)
        g.phase = self.tag + ".mm2b"
        for b in range(NB):
            bank = self.mbank()

            def mm1(e, b=b, bank=bank):
                ins = None
                i = 0
                for (s, _, j0, nj) in units:
                    for jj in range(nj):
                        ins = e.matmul(self.ps[:, bank, :], lhsT=in_chunk(j0 + jj, b), rhs=self.wring[:, s, jj, :],
                                       start=(i == 0), stop=(i == tot - 1))
                        i += 1
                return ins
            g.op("pe", mm1, reads=[k for u in units for k in u[1]] + in_keys(b), writes=[("ps", bank)])
            self.post_block(b, held[b], bank, gidx)
            if on_done is not None:
                on_done(b)

    def conv_pre(self, L):
        g = self.g
        r0 = 0
        g.op("pool", lambda e: e.memset(self.tres[:], 0.0), writes=[("tres", 0), ("tres", 1)])
        g.op("sp", lambda e: e.dma_start(out=self.tres[0:2, :], in_=self.xa[r0:r0 + 2, :]), reads=[("tres", 0), ("tres", 1)],
             writes=[("tres", 0), ("tres", 1)], dma_key=("c", "xpre"))
        g.op("act", lambda e: e.activation(out=self.junk[:], in_=self.tres[:], func=AF.Square, accum_out=self.st(32)),
             reads=[("tres", 0), ("tres", 1)], writes=["sspre"] + [("junk", 0), ("junk", 1)])
        self.rstd_chain(32, 33, 34, 1, ["sspre"], "rstdpre")
        g.op("act", lambda e: e.activation(out=self.xn[:, 1, :], in_=self.tres[:], func=AF.Copy, scale=self.st(34)),
             reads=[("tres", 0), ("tres", 1), "rstdpre"], writes=[("xn", 1)])
        tb = self.tbank()

        def tr(e):
            ins = None
            for kc in range(KC):
                ins = e.transpose(out=self.psb(tb)[:, kc * 128:(kc + 1) * 128], in_=self.xn[:, 1, kc * 128:(kc + 1) * 128], identity=self.ident[:])
            return ins
        g.op("pe", tr, reads=[("xn", 1), "ident"], writes=[("ps", tb)])
        for kc in range(KC):
            g.op("dve", lambda e, kc=kc: e.tensor_scalar(out=self.hpre[:, kc, :], in0=self.psb(tb)[:, kc * 128:kc * 128 + 2],
                                                        scalar1=self.AB[:, L, 0, kc:kc + 1], scalar2=self.AB[:, L, 1, kc:kc + 1],
                                                        op0=ALU.mult, op1=ALU.add),
                 reads=[("ps", tb), ("AB", L, 0), ("AB", L, 1)], writes=[("hpre", kc)])

    def conv_mixer(self, first, mask_carry=False):
        g = self.g
        L = 0
        self.tag = "conv"
        self.ensure_mod(L, 0)
        g.phase = "conv.pre"
        if first:
            self.conv_pre(L)
        if mask_carry:
            ck = [("carry", fc) for fc in range(KC)]
            g.op("dve", lambda e: e.tensor_scalar(out=self.carry[:].rearrange("p k t -> p (k t)"), in0=self.carry[:].rearrange("p k t -> p (k t)"),
                                                  scalar1=self.cmask_sb[:, 0:1], scalar2=None, op0=ALU.mult),
                 reads=ck + ["cmask"], writes=ck)
        self.prenorm(L, 0)
        g.phase = "conv.mm1"
        hk = self.hT_keys()
        for fc in range(KC):
            parts = [(t * 128, 128, self.wsrc(self.conv_w_in, t * D + fc * 128, 128)) for t in range(3)]
            s, keys = self.get_unit(("cin", fc), parts)
            banks = [self.mbank() for _ in range(3)]
            halves = ((0, NT // 2), (NT // 2, NT)) if fc == 0 else ((0, NT),)
            for (c0, c1) in halves:
                for t in range(3):
                    def mm(e, t=t, s=s, bank=banks[t], c0=c0, c1=c1):
                        ins = None
                        for kc in range(KC):
                            ins = e.matmul(self.ps[:, bank, c0:c1], lhsT=self.wring[:, s, kc, t * 128:(t + 1) * 128], rhs=self.hT[:, kc, c0:c1],
                                           start=(kc == 0), stop=(kc == KC - 1))
                        return ins
                    g.op("pe", mm, reads=keys + self.hT_keys(range(c0 // 128, c1 // 128)), writes=[("ps", banks[t])])
            ub = 1 + (fc % 2)
            u = self.arena[:, ub, 0:514]
            if first:
                pp = self.mbank()

                def mmp(e, s=s, pp=pp):
                    ins = None
                    for ti, t in enumerate((1, 2)):
                        for kc in range(KC):
                            ins = e.matmul(self.ps[:, pp, 2 * ti:2 * ti + 2], lhsT=self.wring[:, s, kc, t * 128:(t + 1) * 128], rhs=self.hpre[:, kc, :],
                                           start=(kc == 0), stop=(kc == KC - 1))
                    return ins
                g.op("pe", mmp, reads=keys + [("hpre", kc) for kc in range(KC)], writes=[("ps", pp)])
                g.op("act", lambda e, pp=pp: e.copy(out=self.vpre[:], in_=self.ps[:, pp, 2:4]), reads=[("ps", pp)], writes=["vpre"])
                g.op("dve", lambda e, pp=pp, fc=fc: e.scalar_tensor_tensor(out=self.carry[:, fc, :], in0=self.ps[:, pp, 0:2], scalar=self.cmask_sb[:, 0:1],
                                                                          in1=self.vpre[:], op0=ALU.mult, op1=ALU.mult),
                     reads=[("ps", pp), "vpre", "cmask"], writes=[("carry", fc)])
            g.op("act", lambda e, bank=banks[2]: e.copy(out=self.arena[:, 0, 0:512], in_=self.ps[:, bank, :]), reads=[("ps", banks[2])], writes=[*self.ar(0)])
            g.op("dve", lambda e, u=u, fc=fc: e.tensor_copy(out=u[:, 0:2], in_=self.carry[:, fc, :]), reads=[("carry", fc)], writes=[("ar", ub, 0)])
            g.op("dve", lambda e, u=u, bank=banks[1]: e.tensor_tensor(out=u[:, 2:514], in0=self.ps[:, bank, :], in1=self.arena[:, 0, 0:512], op=ALU.mult),
                 reads=[("ps", banks[1]), *self.ar(0)], writes=[*self.ar(ub)])
            g.op("dve", lambda e, u=u, fc=fc: e.tensor_copy(out=self.carry[:, fc, :], in_=u[:, 512:514]), reads=[*self.ar(ub)], writes=[("carry", fc)])
            t1 = self.arena[:, 3, 0:512]
            t2 = self.arena[:, 4, 0:512]
            g.op("act", lambda e, u=u, fc=fc: e.activation(out=t1, in_=u[:, 2:514], func=AF.Copy, scale=self.cw[:, fc, 2:3]),
                 reads=[*self.ar(ub), "cw"], writes=[*self.ar(3)])
            g.op("dve", lambda e, u=u, fc=fc: e.scalar_tensor_tensor(out=t2, in0=u[:, 1:513], scalar=self.cw[:, fc, 1:2], in1=t1, op0=ALU.mult, op1=ALU.add),
                 reads=[*self.ar(ub), *self.ar(3), "cw"], writes=[*self.ar(4)])
            g.op("dve", lambda e, u=u, fc=fc: e.scalar_tensor_tensor(out=t1, in0=u[:, 0:512], scalar=self.cw[:, fc, 0:1], in1=t2, op0=ALU.mult, op1=ALU.add),
                 reads=[*self.ar(ub), *self.ar(4), "cw"], writes=[*self.ar(3)])
            g.op("dve", lambda e, fc=fc, bank=banks[0]: e.tensor_tensor(out=self.act[:, fc, :], in0=self.ps[:, bank, :], in1=t1, op=ALU.mult),
                 reads=[("ps", banks[0]), *self.ar(3)], writes=self.act_keys([fc]))
        specs = [[(("cout", n), [(0, 512, self.wsrc(self.conv_w_out, n * 512, 512))], 0, KC)] for n in range(2)]
        self.matmul2(specs, lambda j, b: self.act[:, j, b * 128:(b + 1) * 128], lambda b: self.act_keys(range(KC), [b]), 0, nxt=(0, 1))

    def ffn(self, L, on_done=None):
        g = self.g
        self.tag = "ffn%d" % L
        self.prenorm(L, 1)
        g.phase = self.tag + ".mm1"
        hk = self.hT_keys()
        gu = self.ffn_gu[L]
        for jp in range(JC // 2):
            j0 = 2 * jp
            parts = [(0, 256, self.wsrc(gu, j0 * 128, 256)), (256, 256, self.wsrc(gu, DFF + j0 * 128, 256))]
            s, keys = self.get_unit(("gu", L, jp), parts)
            for jj in range(2):
                j = j0 + jj
                pg, pu = self.mbank(), self.mbank()
                halves = ((0, NT // 2), (NT // 2, NT)) if j == 0 else ((0, NT),)
                for (t0, t1) in halves:
                    for which, bank in ((0, pg), (1, pu)):
                        def mm(e, s=s, c0=which * 256 + jj * 128, bank=bank, t0=t0, t1=t1):
                            ins = None
                            for kc in range(KC):
                                ins = e.matmul(self.ps[:, bank, t0:t1], lhsT=self.wring[:, s, kc, c0:c0 + 128], rhs=self.hT[:, kc, t0:t1],
                                               start=(kc == 0), stop=(kc == KC - 1))
                            return ins
                        g.op("pe", mm, reads=keys + self.hT_keys(range(t0 // 128, t1 // 128)), writes=[("ps", bank)])
                ab = j % 3
                g.op("act", lambda e, pg=pg, ab=ab: e.activation(out=self.arena[:, ab, 0:512], in_=self.ps[:, pg, :], func=AF.Silu),
                     reads=[("ps", pg)], writes=[*self.ar(ab)])
                g.op("dve", lambda e, pu=pu, ab=ab, j=j: e.tensor_tensor(out=self.act[:, j, :], in0=self.ps[:, pu, :], in1=self.arena[:, ab, 0:512], op=ALU.mult),
                     reads=[("ps", pu), *self.ar(ab)], writes=self.act_keys([j]))
        wd = self.ffn_down[L]
        specs = [[(("down", L, n, gi), [(0, 512, self.wsrc(wd, n * 512, 512, r0=j0, nk=nj))], j0, nj)
                  for gi, (j0, nj) in enumerate(((0, 8), (8, 8), (16, 6)))] for n in range(2)]
        self.matmul2(specs, lambda j, b: self.act[:, j, b * 128:(b + 1) * 128], lambda b: self.act_keys(range(JC), [b]), 2 * L + 1, on_done=on_done, nxt=((1, 0) if (L == 0 and self.do_l1) else None))

    def attn_kv(self, kv_only):
        g = self.g
        g.phase = "attn.qkv"
        hk = self.hT_keys()
        W = self.attn_w_qkv
        g.op("dve", lambda e: e.tensor_copy(out=self.KT[:, :, 0:NT], in_=self.KT[:, :, NT:2 * NT]),
             reads=[("KT", fc, 1) for fc in range(KC)], writes=[("KT", fc, 0) for fc in range(KC)])
        g.op("dve", lambda e: e.tensor_copy(out=self.V[:, 0:NB, :], in_=self.V[:, NB:2 * NB, :]),
             reads=[("V", NB + b, n) for b in range(NB) for n in range(2)], writes=[("V", b, n) for b in range(NB) for n in range(2)])
        for n in range(2):
            s, keys = self.get_unit(("v", n), [(0, 512, self.wsrc(W, 2 * D + n * 512, 512))])
            for b in range(NB):
                bank = self.mbank()

                def mmv(e, s=s, b=b, bank=bank):
                    ins = None
                    for kc in range(KC):
                        ins = e.matmul(self.ps[:, bank, :], lhsT=self.hT[:, kc, b * 128:(b + 1) * 128], rhs=self.wring[:, s, kc, :],
                                       start=(kc == 0), stop=(kc == KC - 1))
                    return ins
                g.op("pe", mmv, reads=keys + self.hT_keys([b]), writes=[("ps", bank)])
                g.op("dve", lambda e, b=b, n=n, bank=bank: e.tensor_copy(out=self.V[:, NB + b, n * 512:(n + 1) * 512], in_=self.ps[:, bank, :]),
                     reads=[("ps", bank)], writes=[("V", NB + b, n)])
        for which in ((1,) if kv_only else (1, 0)):
            for u in range(2):
                s, keys = self.get_unit(("qk", which, u), [(0, 512, self.wsrc(W, which * D + u * 512, 512))])
                for fcl in range(4):
                    fc = u * 4 + fcl
                    bank = self.mbank()

                    def mm(e, s=s, fcl=fcl, bank=bank):
                        ins = None
                        for kc in range(KC):
                            ins = e.matmul(self.ps[:, bank, :], lhsT=self.wring[:, s, kc, fcl * 128:(fcl + 1) * 128], rhs=self.hT[:, kc, :],
                                           start=(kc == 0), stop=(kc == KC - 1))
                        return ins
                    g.op("pe", mm, reads=keys + hk, writes=[("ps", bank)])
                    if which == 1:
                        g.op("act", lambda e, fc=fc, bank=bank: e.copy(out=self.KT[:, fc, NT:2 * NT], in_=self.ps[:, bank, :]),
                             reads=[("ps", bank)], writes=[("KT", fc, 1)])
                    else:
                        g.op("act", lambda e, fc=fc, bank=bank: e.activation(out=self.act[:, 8 + fc, :], in_=self.ps[:, bank, :], func=AF.Copy, scale=0.125),
                             reads=[("ps", bank)], writes=self.act_keys([8 + fc]))

    def attention(self, mask_halo):
        g = self.g
        g.phase = "attn.heads"
        LM, LT, LP = 1, 3, 5
        n_items = NB * NH
        pvb = 2

        def rs_col(qb, h):
            return 40 + 16 * (qb % 2) + h

        def stage_s(i):
            qb, h = divmod(i, NH)
            q0 = qb * 128
            fc = h // 2
            sp_ = 4 + 2 * (i % 2)
            nh = NT - q0 if mask_halo else 0

            def mm(e):
                qT = self.act[:, 8 + fc, q0:q0 + 128] if h % 2 == 0 else self.hT[:, fc, q0:q0 + 128]
                e.matmul(self.ps[:, sp_, :], lhsT=qT, rhs=self.KT[:, fc, q0:q0 + 512], start=True, stop=(nh == 0))
                if nh:
                    e.matmul(self.ps[:, sp_, 0:nh], lhsT=self.ones2[:], rhs=self.mrow[:, 0:nh], start=False, stop=True)
                return e.matmul(self.ps[:, sp_ + 1, 0:128], lhsT=qT, rhs=self.KT[:, fc, q0 + 512:q0 + 640], start=True, stop=True)
            rk = [("act", 8 + fc, qb) if h % 2 == 0 else ("hT", fc, qb), ("KT", fc, 0), ("KT", fc, 1)] + (["ones2", "mrow0"] if nh else [])
            g.op("pe", mm, reads=rk, writes=[("ps", sp_), ("ps", sp_ + 1)])
            tS, tk = self.tsbuf(i % 3)
            sview = self.ps[:, sp_:sp_ + 2, :].rearrange("p a b -> p (a b)")[:, 0:640]
            g.op("dve", lambda e: e.tensor_tensor(out=tS, in0=sview, in1=self.bias[:, h, :], op=ALU.add),
                 reads=[("ps", sp_), ("ps", sp_ + 1), "bias"], writes=tk)

        def stage_m(i):
            qb, h = divmod(i, NH)
            tb = i % 3
            tS, tk = self.tsbuf(tb)
            g.op("dve", lambda e: e.reduce_max(out=self.st(36 + tb), in_=tS, axis=AX.X, negate=True), reads=tk, writes=[("nm", tb)])
            p = self.pbuf(tb)
            g.op("act", lambda e: e.activation(out=p, in_=tS, func=AF.Exp, bias=self.st(36 + tb), scale=1.0, accum_out=self.st(rs_col(qb, h))),
                 reads=tk + [("nm", tb)], writes=[self.pkey(tb), ("rs", qb % 2, h)])

        def stage_t(i):
            tb = i % 3
            p = self.pbuf(tb)
            bank = self.tbank()

            def tr(e):
                ins = None
                for j in range(5):
                    ins = e.transpose(out=self.psb(bank)[:, j * 128:(j + 1) * 128], in_=p[:, j * 128:(j + 1) * 128], identity=self.ident[:])
                return ins
            g.op("pe", tr, reads=[self.pkey(tb), "ident"], writes=[("ps", bank)])
            pT = self.ptbuf(tb)
            g.op("act", lambda e: e.copy(out=pT, in_=self.psb(bank)[:, 0:640]), reads=[("ps", bank)], writes=[self.ptkey(tb)])

        def stage_pv(i):
            qb, h = divmod(i, NH)
            tb = i % 3
            pT = self.ptbuf(tb)
            bank = pvb + h // 8
            c0 = (h % 8) * 64

            def mm(e):
                ins = None
                for j in range(5):
                    ins = e.matmul(self.ps[:, bank, c0:c0 + 64], lhsT=pT[:, j * 128:(j + 1) * 128], rhs=self.V[:, qb + j, h * 64:(h + 1) * 64],
                                   start=(j == 0), stop=(j == 4))
                return ins
            g.op("pe", mm, reads=[self.ptkey(tb)] + [("V", qb + j, h // 8) for j in range(5)], writes=[("ps", bank)])
            if h == NH - 1:
                epilogue_a(qb)
                pending.append((i + LP + 2, qb))

        pending = []

        def epilogue_a(qb):
            c = 40 + 16 * (qb % 2)
            g.op("dve", lambda e: e.reciprocal(out=self.st(76, 92), in_=self.st(c, c + 16)), reads=[("rs", qb % 2, h) for h in range(NH)], writes=["rr"])
            pvv = self.ps[:, pvb:pvb + 2, :].rearrange("p a (h d) -> p (a h) d", d=64)
            g.op("dve", lambda e: e.tensor_tensor(out=self.attn_sb[:].rearrange("p (h d) -> p h d", d=64), in0=pvv,
                                                  in1=self.st(76, 92).unsqueeze(2).to_broadcast([128, NH, 64]), op=ALU.mult),
                 reads=[("ps", pvb), ("ps", pvb + 1), "rr"], writes=["attn_sb"])

        def epilogue_b(qb):
            q0 = qb * 128
            bank = self.tbank()

            def tra(e):
                ins = None
                for fc in range(KC):
                    ins = e.transpose(out=self.psb(bank)[:, fc * 128:(fc + 1) * 128], in_=self.attn_sb[:, fc * 128:(fc + 1) * 128], identity=self.ident[:])
                return ins
            g.op("pe", tra, reads=["attn_sb", "ident"], writes=[("ps", bank)])
            g.op("act", lambda e: e.copy(out=self.act[:, 0:KC, q0:q0 + 128], in_=self.psb(bank).rearrange("p (k t) -> p k t", k=KC)),
                 reads=[("ps", bank)], writes=self.act_keys(range(KC), [qb]))

        for step in range(n_items + LP):
            if step < n_items:
                stage_s(step)
            if LM <= step < n_items + LM:
                stage_m(step - LM)
            if LT <= step < n_items + LT:
                stage_t(step - LT)
            if LP <= step:
                stage_pv(step - LP)
            while pending and pending[0][0] <= step:
                epilogue_b(pending.pop(0)[1])
        while pending:
            epilogue_b(pending.pop(0)[1])

    def attn_mixer(self, kv_only, mask_halo):
        L = 1
        self.tag = "attn"
        self.prenorm(L, 0)
        self.attn_kv(kv_only)
        if kv_only:
            return
        g = self.g
        g.phase = "attn.qmask"
        for fc in range(KC):
            qk = self.act_keys([8 + fc])
            hk = [("hT", fc, b) for b in range(NB)]
            g.op("dve", lambda e, fc=fc: e.tensor_copy(out=self.hT[64:128, fc, :], in_=self.act[64:128, 8 + fc, :]), reads=qk + hk, writes=hk)
            g.op("pool", lambda e, fc=fc: e.memset(self.hT[0:64, fc, :], 0.0), reads=hk, writes=hk)
            g.op("pool", lambda e, fc=fc: e.memset(self.act[64:128, 8 + fc, :], 0.0), reads=qk, writes=qk)
        self.attention(mask_halo)
        specs = [[(("aout", n), [(0, 512, self.wsrc(self.attn_w_out, n * 512, 512))], 0, KC)] for n in range(2)]
        self.matmul2(specs, lambda j, b: self.act[:, j, b * 128:(b + 1) * 128], lambda b: self.act_keys(range(KC), [b]), 2, nxt=(1, 1))

    def build(self):
        g = self.g
        self.prologue()
        if self.do_l1:
            g.op("pool", lambda e: e.memset(self.KT[:], 0.0), writes=[("KT", fc, h) for fc in range(KC) for h in range(2)])
            g.op("pool", lambda e: e.memset(self.V[:], 0.0), writes=[("V", b, n) for b in range(2 * NB) for n in range(2)])
        xoff = 2 if self.do_l0 else 0

        def load_block(t, b):
            r = xoff + t * NT + b * 128
            sl = self.xs(b, t)
            g.op("sp", lambda e: e.dma_start(out=self.x_sb[:, sl, :], in_=self.xa[r:r + 128, :]), writes=[("x", sl)], dma_key=("xld", sl))

        def store_block(t, b):
            r = (t - self.first_out_tile) * NT + b * 128
            sl = self.xs(b, t)
            g.op("sp", lambda e: e.dma_start(out=self.out[r:r + 128, :], in_=self.x_sb[:, sl, :]), reads=[("x", sl)], writes=[("out", b)],
                 dma_key=("xst", sl))

        def tile_tail(t):
            def cb(b):
                g.phase = "io"
                if t >= self.first_out_tile:
                    store_block(t, b)
                if t + 1 < self.n_tiles and b + 1 < NB:
                    load_block(t + 1, b + 1)
            return cb

        for b in range(NB):
            load_block(0, b)
        for t in range(self.n_tiles):
            self.tile_t = t
            if t + 1 < self.n_tiles:
                load_block(t + 1, 0)
            kv_only = self.do_l1 and t == 0
            if self.do_l0:
                self.conv_mixer(first=(t == 0), mask_carry=(self.mode == "fused" and t == self.first_out_tile))
                self.ffn(0, on_done=None if self.do_l1 else tile_tail(t))
            if self.do_l1:
                self.attn_mixer(kv_only, mask_halo=(t == 1))
                if kv_only:
                    for b in range(NB):
                        tile_tail(t)(b)
                else:
                    self.ffn(1, on_done=tile_tail(t))
        g.op("sp", lambda e: None, reads=[("out", b) for b in range(NB)])
        g.finalize(self.ctx)
        self.ctx.close()
        return self.nc


def _bias_tile(rel_bias):
    qi = np.arange(128)[:, None]
    kj = np.arange(640)[None, :]
    dist = qi + 512 - kj
    idx = np.clip(dist, -256, 256) + 256
    cq, ck = qi // 64, kj // 64
    vis = (ck >= cq) & (ck <= cq + 8)
    gathered = rel_bias[:, idx]
    full = np.where(vis[None], gathered, np.float32(NEG)).astype(np.float32)
    return np.ascontiguousarray(full.transpose(1, 0, 2))


def _common_inputs(c_b, ada_w, ada_b, norm_gains, conv_w_in, conv_w, conv_w_out, attn_w_qkv, attn_rel_bias, attn_w_out,
                   ffn_w_gate_up, ffn_w_down, seq_start):
    f = np.float32
    hm = np.zeros((2, 512), dtype=ml_dtypes.bfloat16)
    if seq_start:
        hm[:] = NEG
    return {
        "cvec": np.ascontiguousarray(c_b.reshape(KC, 128).T).astype(f),
        "ada_w": ada_w, "ada_b": ada_b,
        "ada_bT": np.ascontiguousarray(ada_b.reshape(2, 48, 128).transpose(0, 2, 1)),
        "gains": norm_gains,
        "gainsT": np.ascontiguousarray(norm_gains.reshape(2, 4, KC, 128).transpose(0, 1, 3, 2)),
        "conv_w_in": conv_w_in[0],
        "conv_wT": np.ascontiguousarray(conv_w[0].reshape(3, KC, 128).transpose(2, 1, 0)),
        "conv_w_out": conv_w_out[0],
        "attn_w_qkv": attn_w_qkv[0], "attn_w_out": attn_w_out[0],
        "ffn_gu": ffn_w_gate_up, "ffn_down": ffn_w_down,
        "bias_t": _bias_tile(attn_rel_bias[0]),
        "hmask": hm,
        "cmask": np.full((128, 1), 0.0 if seq_start else 1.0, dtype=f),
        "idbf": np.eye(128).astype(ml_dtypes.bfloat16),
    }


_NC_CACHE = {}


def _program(mode):
    if mode not in _NC_CACHE:
        b = Builder(mode)
        b.declare()
        _NC_CACHE[mode] = b.build()
    return _NC_CACHE[mode]


def _rows(xb, lo, hi):
    out = np.zeros((hi - lo, xb.shape[1]), dtype=np.float32)
    a = max(lo, 0)
    if hi > a:
        out[a - lo:] = xb[a:hi]
    return out


def kernel(x, c, ada_w, ada_b, norm_gains, conv_w_in, conv_w, conv_w_out, attn_w_qkv, attn_rel_bias, attn_w_out,
           ffn_w_gate_up, ffn_w_down):
    args = [np.asarray(a, dtype=np.float32) for a in (ada_w, ada_b, norm_gains, conv_w_in, conv_w, conv_w_out, attn_w_qkv,
                                                     attn_rel_bias, attn_w_out, ffn_w_gate_up, ffn_w_down)]
    x = np.asarray(x, dtype=np.float32)
    c = np.asarray(c, dtype=np.float32)
    n_cores = 8
    in_maps = []
    for k in range(n_cores):
        b, s0 = k // 4, (k % 4) * TOK_CORE
        m = _common_inputs(c[b], *args, seq_start=(s0 == 0))
        m["xa"] = _rows(x[b], s0 - HALO - 2, s0 + TOK_CORE)
        in_maps.append(m)
    res = run_bass_kernel_spmd(_program("fused"), in_maps, core_ids=list(range(n_cores)))
    out = np.empty_like(x)
    for k in range(n_cores):
        b, s0 = k // 4, (k % 4) * TOK_CORE
        out[b, s0:s0 + TOK_CORE] = res.results[k]["out"]
    return out
```

```python
from contextlib import ExitStack

import ml_dtypes
import numpy as np

import concourse.bass as bass
import concourse.mybir as mybir
from concourse.bass_utils import run_bass_kernel_spmd

F32 = mybir.dt.float32
BF16 = mybir.dt.bfloat16
AF = mybir.ActivationFunctionType
ALU = mybir.AluOpType
AX = mybir.AxisListType

D = 1024
KC = 8
NT = 512
NB = 4
DFF = 2816
JC = 22
NH = 16
TOK_CORE = 2048
HALO = 512
EPS = 1e-6
NEG = -30000.0
NSLOT = 5
N_SCRATCH = 52

ENGS = ["pe", "act", "dve", "pool", "sp"]
EIDX = {e: i for i, e in enumerate(ENGS)}


class Op:
    __slots__ = ("eng", "fn", "reads", "writes", "dma_key", "deps", "idx", "signal", "signo", "vc", "dvc",
                 "dma_val", "waits", "label")

    def __init__(self, eng, fn, reads, writes, dma_key):
        self.eng = eng
        self.fn = fn
        self.reads = reads
        self.writes = writes
        self.dma_key = dma_key
        self.deps = []
        self.idx = 0
        self.signal = False
        self.signo = 0
        self.vc = None
        self.dvc = None
        self.dma_val = 0
        self.waits = []


class Graph:
    def __init__(self, nc):
        self.nc = nc
        self.ops = []
        self.last_writer = {}
        self.readers = {}
        self.dma_count = {}
        self.phase = ""

    def op(self, eng, fn, reads=(), writes=(), dma_key=None):
        o = Op(eng, fn, tuple(reads), tuple(writes), dma_key)
        o.label = self.phase
        deps = {}
        for k in o.reads:
            w = self.last_writer.get(k)
            if w is not None:
                deps[id(w)] = (w, "raw")
            if type(k) is tuple and k[0] == "ps":
                for r in self.readers.get(k, ()):
                    if r.eng != eng and id(r) not in deps:
                        deps[id(r)] = (r, "rar")
        for k in o.writes:
            w = self.last_writer.get(k)
            if w is not None and id(w) not in deps:
                deps[id(w)] = (w, "waw")
            for r in self.readers.get(k, ()):
                if id(r) not in deps:
                    deps[id(r)] = (r, "war")
        for d, kind in deps.values():
            if d is o:
                continue
            same = (d.eng == o.eng) and d.dma_key is None and o.dma_key is None
            if same and o.eng == "pe":
                continue
            o.deps.append(d)
        for k in o.writes:
            self.last_writer[k] = o
            self.readers[k] = []
        for k in o.reads:
            self.readers.setdefault(k, []).append(o)
        if dma_key is not None:
            self.dma_count[dma_key] = self.dma_count.get(dma_key, 0) + 1
            o.dma_val = 16 * self.dma_count[dma_key]
        self.ops.append(o)
        return o

    def finalize(self, ctx):
        nc = self.nc
        ne = len(ENGS)
        cnt = [0] * ne
        seen = [[0] * ne for _ in range(ne)]
        seen_dma = [dict() for _ in range(ne)]
        per_eng = {e: [] for e in ENGS}
        order = {id(o): i for i, o in enumerate(self.ops)}
        for o in self.ops:
            e = EIDX[o.eng]
            if o.dma_key is None:
                cnt[e] += 1
                o.idx = cnt[e]
            st = seen[e]
            sd = seen_dma[e]
            for d in sorted(o.deps, key=lambda d: -order[id(d)]):
                if d.dma_key is not None:
                    if sd.get(d.dma_key, 0) >= d.dma_val:
                        continue
                    o.waits.append(("dma", d.dma_key, d.dma_val))
                    sd[d.dma_key] = d.dma_val
                else:
                    p = EIDX[d.eng]
                    if st[p] >= d.idx:
                        continue
                    d.signal = True
                    o.waits.append(("eng", d))
                for i in range(ne):
                    if d.vc[i] > st[i]:
                        st[i] = d.vc[i]
                for k2, v2 in d.dvc.items():
                    if sd.get(k2, 0) < v2:
                        sd[k2] = v2
            o.vc = list(st)
            o.dvc = dict(sd)
            if o.dma_key is None:
                o.vc[e] = o.idx
            per_eng[o.eng].append(o)
        signo = [0] * ne
        for o in self.ops:
            if o.dma_key is None and o.signal:
                e = EIDX[o.eng]
                signo[e] += 1
                o.signo = signo[e]
        esem = {e: ctx.enter_context(nc.semaphore("s_" + e)) for e in ENGS}
        dsem = {}
        for i, k in enumerate(self.dma_count):
            dsem[k] = ctx.enter_context(nc.semaphore("d%d" % i))
        self.n_sems = len(esem) + len(dsem)

        def emit_stream(engname, h):
            for o in per_eng[engname]:
                for w in o.waits:
                    if w[0] == "dma":
                        h.wait_ge(dsem[w[1]], w[2])
                    else:
                        h.wait_ge(esem[w[1].eng], w[1].signo)
                inst = o.fn(h)
                if inst is None:
                    continue
                if o.dma_key is not None:
                    inst.then_inc(dsem[o.dma_key], 16)
                elif o.signal:
                    inst.then_inc(esem[engname], 1)

        with nc.Block() as block:
            @block.tensor
            def _(eng):
                emit_stream("pe", nc.tensor)

            @block.scalar
            def _(eng):
                emit_stream("act", nc.scalar)

            @block.vector
            def _(eng):
                emit_stream("dve", nc.vector)

            @block.gpsimd
            def _(eng):
                emit_stream("pool", nc.gpsimd)

            @block.sync
            def _(eng):
                emit_stream("sp", nc.sync)


class Builder:
    def __init__(self, mode="fused", n_tiles=None):
        self.mode = mode
        self.do_l0 = mode in ("fused", "l0")
        self.do_l1 = mode in ("fused", "l1")
        if mode == "l0":
            self.n_tiles = 4 if n_tiles is None else n_tiles
            self.n_in_rows = 2 + self.n_tiles * NT
            self.first_out_tile = 0
        else:
            self.n_tiles = 5 if n_tiles is None else n_tiles
            self.n_in_rows = (2 if mode == "fused" else 0) + self.n_tiles * NT
            self.first_out_tile = 1
        self.n_out_rows = (self.n_tiles - self.first_out_tile) * NT
        self.layers = ([0] if self.do_l0 else []) + ([1] if self.do_l1 else [])
        self.nc = bass.Bass("TRN2", target_bir_lowering=False)
        self.ctx = ExitStack()
        self.bank_rr = 0
        self.tbank_rr = 0
        self.slot_rr = 0
        self.unit_idx = {}
        self.unit_uses = {}
        self.tile_t = 0
        self.mod_done = set()

    def dram_in(self, name, shape, dt=F32):
        return self.nc.dram_tensor(name, list(shape), dt, kind="ExternalInput").ap()

    def sb(self, name, shape, dt):
        return self.ctx.enter_context(self.nc.sbuf_tensor(name, list(shape), dt))

    def declare(self):
        nc = self.nc
        self.xa = self.dram_in("xa", [self.n_in_rows, D])
        self.cvec = self.dram_in("cvec", [128, KC])
        self.ada_w = self.dram_in("ada_w", [2, D, 6 * D])
        self.ada_b = self.dram_in("ada_b", [2, 6 * D])
        self.ada_bT = self.dram_in("ada_bT", [2, 128, 48])
        self.gains = self.dram_in("gains", [2, 4, D])
        self.gainsT = self.dram_in("gainsT", [2, 4, 128, KC])
        self.conv_w_in = self.dram_in("conv_w_in", [D, 3 * D])
        self.conv_wT = self.dram_in("conv_wT", [128, KC, 3])
        self.conv_w_out = self.dram_in("conv_w_out", [D, D])
        self.attn_w_qkv = self.dram_in("attn_w_qkv", [D, 3 * D])
        self.attn_w_out = self.dram_in("attn_w_out", [D, D])
        self.ffn_gu = self.dram_in("ffn_gu", [2, D, 2 * DFF])
        self.ffn_down = self.dram_in("ffn_down", [2, DFF, D])
        self.bias_t = self.dram_in("bias_t", [128, NH, 640])
        self.hmask = self.dram_in("hmask", [2, 512], BF16)
        self.cmask = self.dram_in("cmask", [128, 1])
        self.idbf = self.dram_in("idbf", [128, 128], BF16)
        self.out = nc.dram_tensor("out", [self.n_out_rows, D], F32, kind="ExternalOutput").ap()
        self.scratch = nc.dram_tensor("wscratch", [N_SCRATCH, 128, KC * 512], BF16, kind="Internal").ap()

        sb = self.sb
        self.x_sb = sb("x_sb", [128, NB + 1, D], F32)
        self.xn = sb("xn", [128, 2, D], BF16)
        self.junk = sb("junk", [128, D], BF16)
        self.hT = sb("hT", [128, KC, NT], BF16)
        self.hpre = sb("hpre", [128, KC, 2], BF16)
        self.act = sb("act", [128, JC, NT], BF16)
        self.tres = sb("tres", [128, D], F32)
        self.arena = sb("arena", [128, 5, 640], F32)
        self.G = sb("G", [128, 4, D], F32)
        self.AB = sb("AB", [128, 2, 4, KC], F32)
        self.modfm = sb("modfm", [128, 4, KC], F32)
        self.gT = sb("gT", [128, 2, KC], F32)
        self.bT = sb("bT", [128, 48], F32)
        self.stats = sb("stats", [128, 96], F32)
        self.ident = sb("ident", [128, 128], BF16)
        self.c_sb = sb("c_sb", [128, KC], F32)
        self.c_bf2 = sb("c_bf2", [128, KC, 2], BF16)
        self.cT_rep = sb("cT_rep", [128, KC, 128], BF16)
        self.cmask_sb = sb("cmask_sb", [128, 1], F32)
        self.wring = sb("wring", [128, NSLOT, KC, 512], BF16)
        if self.do_l0:
            self.cw = sb("cw", [128, KC, 3], F32)
            self.carry = sb("carry", [128, KC, 2], F32)
            self.vpre = sb("vpre", [128, 2], F32)
        if self.do_l1:
            self.KT = sb("KT", [128, KC, 2 * NT], BF16)
            self.V = sb("V", [128, 2 * NB, D], BF16)
            self.bias = sb("bias", [128, NH, 640], F32)
            self.attn_sb = sb("attn_sb", [128, D], BF16)
            self.ones2 = sb("ones2", [128, 128], BF16)
            self.mrow = sb("mrow", [128, 512], BF16)
        self.ps = self.ctx.enter_context(nc.psum_tensor("ps", [128, 8, 512], F32))
        self.g = Graph(nc)

    def psb(self, bank):
        return self.ps[:, bank, :].bitcast(BF16)

    def arb(self, i):
        return self.arena[:, i, :].bitcast(BF16)

    def tsbuf(self, i):
        if i < 2:
            return self.arena[:, i, :], self.ar(i)
        return self.tres[:, 0:640], [("tres", 0), ("tres", 1)]

    def pbuf(self, i):
        c, hf = self.pkey(i)[1:]
        return self.arb(c)[:, hf * 640:(hf + 1) * 640]

    @staticmethod
    def pkey(i):
        return ("ar", 2 + i // 2, i % 2)

    def ptbuf(self, i):
        c, hf = self.ptkey(i)[1:]
        return self.arb(c)[:, hf * 640:(hf + 1) * 640]

    @staticmethod
    def ptkey(i):
        j = i + 1
        return ("ar", 3 + j // 2, j % 2)

    @staticmethod
    def ar(i):
        return [("ar", i, 0), ("ar", i, 1)]

    def xs(self, b, t=None):
        return (b - (self.tile_t if t is None else t)) % (NB + 1)

    def tbank(self):
        b = self.tbank_rr
        self.tbank_rr ^= 1
        return b

    def mbank(self):
        b = 2 + self.bank_rr
        self.bank_rr = (self.bank_rr + 1) % 6
        return b

    def mpair(self):
        if self.bank_rr % 2:
            self.bank_rr = (self.bank_rr + 1) % 6
        b = 2 + self.bank_rr
        self.bank_rr = (self.bank_rr + 2) % 6
        return b

    def st(self, c0, c1=None):
        return self.stats[:, c0:(c0 + 1 if c1 is None else c1)]

    def get_unit(self, name, parts, nk=KC):
        g = self.g
        s = self.slot_rr
        self.slot_rr = (self.slot_rr + 1) % NSLOT
        slot = self.wring[:, s]
        keys = [("w", s, p) for p in range(3)]
        skey = ("wslot", s)
        width = max(c0 + n for c0, n, _ in parts)
        ui, scr = None, None
        if name[0] != "ada":
            if name not in self.unit_idx:
                self.unit_idx[name] = len(self.unit_idx)
            ui = self.unit_idx[name]
            assert ui < N_SCRATCH, ui
            scr = self.scratch[ui].rearrange("p (k c) -> p k c", k=KC)[:, 0:nk, 0:width]
        uses = self.unit_uses.get(name, 0) + 1
        self.unit_uses[name] = uses
        if name[0] == "ada" or uses <= 2:
            used = []
            for pi, (c0, n, src) in enumerate(parts):
                dst = slot[:, 0:nk, c0:c0 + n]
                g.op("pool", lambda e, dst=dst, src=src: e.dma_start(out=dst, in_=src),
                     writes=([skey] if pi == 0 else []) + [keys[pi]], dma_key=("wc", s, pi))
                used.append(keys[pi])
            if name[0] != "ada" and uses == 2:
                g.op("sp", lambda e, scr=scr, s=s, nk=nk, width=width: e.dma_start(out=scr, in_=self.wring[:, s, 0:nk, 0:width]),
                     reads=[skey] + used, writes=[("scr", ui)], dma_key=("wb", s))
            return s, [skey] + used
        g.op("sp", lambda e, scr=scr, s=s, nk=nk, width=width: e.dma_start(out=self.wring[:, s, 0:nk, 0:width], in_=scr),
             reads=[("scr", ui)], writes=[skey] + keys, dma_key=("wl", s))
        return s, [skey] + keys

    def wsrc(self, w2d, c0, n, r0=0, nk=KC):
        return w2d.rearrange("(k p) n -> p k n", p=128)[:, r0:r0 + nk, c0:c0 + n]

    def prologue(self):
        g, nc = self.g, self.nc
        ld = lambda dst, src, key: g.op("sp", lambda e: e.dma_start(out=dst, in_=src), writes=[key], dma_key=("c", key))
        ld(self.ident[:], self.idbf, "ident")
        ld(self.c_sb[:], self.cvec, "c_sb")
        ld(self.cmask_sb[:], self.cmask, "cmask")
        if self.do_l0:
            ld(self.cw[:], self.conv_wT, "cw")
        if self.do_l1:
            ld(self.bias[:], self.bias_t, "bias")
            g.op("pool", lambda e: e.memset(self.mrow[:], 0.0), writes=["mrow0"])
            g.op("sp", lambda e: e.dma_start(out=self.mrow[0:1, :], in_=self.hmask[0:1, :]), reads=["mrow0"], writes=["mrow0"], dma_key=("c", "mrow0"))
            g.op("pool", lambda e: e.memset(self.ones2[:], 0.0), writes=["ones2"])
            g.op("pool", lambda e: e.memset(self.ones2[0:1, :], 1.0), reads=["ones2"], writes=["ones2"])
        g.op("pool", lambda e: e.memset(self.stats[:, 72:76], -0.5), writes=["mhalf"])
        g.op("act", lambda e: e.activation(out=self.c_sb[:], in_=self.c_sb[:], func=AF.Silu), reads=["c_sb"], writes=["c_act"])
        g.op("dve", lambda e: e.tensor_copy(out=self.c_bf2[:], in_=self.c_sb[:].unsqueeze(2).to_broadcast([128, KC, 2])),
             reads=["c_act"], writes=["c_bf2"])
        g.op("dve", lambda e: e.tensor_copy(out=self.cT_rep[:], in_=self.c_sb[:].unsqueeze(2).to_broadcast([128, KC, 128])),
             reads=["c_act"], writes=["cT_rep"])

    def ensure_mod(self, L, sub):
        if (L, sub) in self.mod_done:
            return
        self.mod_done.add((L, sub))
        g = self.g
        g.phase = "mod"
        adaw = self.ada_w[L]
        g.op("sp", lambda e: e.dma_start(out=self.bT[:], in_=self.ada_bT[L]), writes=["bT"], dma_key=("c", "bT"))
        for sub, (v_sh, v_sc, v_g, gi0, gi1) in ((sub, ((0, 1, 2, 0, 1), (3, 4, 5, 2, 3))[sub]),):
            pb = self.mbank()
            for vi, v in enumerate((v_sh, v_sc)):
                for half in range(2):
                    c0 = v * D + half * 512
                    s, keys = self.get_unit(("ada", L, v, half), [(0, 512, self.wsrc(adaw, c0, 512))])

                    def mm(e, s=s, vi=vi, half=half, pb=pb):
                        ins = None
                        for fcl in range(4):
                            col = (vi * KC + half * 4 + fcl) * 2
                            for kc in range(KC):
                                ins = e.matmul(self.ps[:, pb, col:col + 2], lhsT=self.wring[:, s, kc, fcl * 128:(fcl + 1) * 128],
                                               rhs=self.c_bf2[:, kc, :], start=(kc == 0), stop=(kc == KC - 1))
                        return ins
                    g.op("pe", mm, reads=keys + ["c_bf2"], writes=[("ps", pb)])
            gsrc = self.gainsT[L, gi0]
            g.op("sp", lambda e, gsrc=gsrc, sub=sub: e.dma_start(out=self.gT[:, sub, :], in_=gsrc), writes=[("gT", sub)], dma_key=("c", "gT", sub))
            psv = self.ps[:, pb, 0:32].rearrange("p (v k two) -> p v k two", v=2, two=2)[:, :, :, 0]

            def ev(e, psv=psv, v_sh=v_sh, sub=sub):
                bview = self.bT[:, v_sh * KC:(v_sh + 2) * KC].rearrange("p (v k) -> p v k", v=2)
                return e.tensor_tensor(out=self.modfm[:, 2 * sub:2 * sub + 2, :], in0=psv, in1=bview, op=ALU.add)
            g.op("dve", ev, reads=[("ps", pb), "bT"], writes=[("modfm", sub)])
            g.op("dve", lambda e, sub=sub, L=L: e.scalar_tensor_tensor(out=self.AB[:, L, 2 * sub, :], in0=self.modfm[:, 2 * sub + 1, :], scalar=1.0,
                                                                      in1=self.gT[:, sub, :], op0=ALU.add, op1=ALU.mult),
                 reads=[("modfm", sub), ("gT", sub)], writes=[("AB", L, 2 * sub)])
            g.op("dve", lambda e, sub=sub, L=L: e.tensor_copy(out=self.AB[:, L, 2 * sub + 1, :], in_=self.modfm[:, 2 * sub, :]),
                 reads=[("modfm", sub)], writes=[("AB", L, 2 * sub + 1)])
            for half in range(2):
                c0 = v_g * D + half * 512
                s, keys = self.get_unit(("ada", L, v_g, half), [(0, 512, self.wsrc(adaw, c0, 512))])
                pg = self.mbank()

                def mmg(e, s=s, pg=pg):
                    ins = None
                    for kc in range(KC):
                        ins = e.matmul(self.ps[:, pg, :], lhsT=self.cT_rep[:, kc, :], rhs=self.wring[:, s, kc, :],
                                       start=(kc == 0), stop=(kc == KC - 1))
                    return ins
                g.op("pe", mmg, reads=keys + ["cT_rep"], writes=[("ps", pg)])
                bsrc = self.ada_b[L, c0:c0 + 512].partition_broadcast(128)
                gsrc2 = self.gains[L, gi1, half * 512:(half + 1) * 512].partition_broadcast(128)
                g.op("sp", lambda e, bsrc=bsrc: e.dma_start(out=self.arena[:, 0, 0:512], in_=bsrc), writes=[*self.ar(0)], dma_key=("c", "ar0"))
                g.op("sp", lambda e, gsrc2=gsrc2: e.dma_start(out=self.arena[:, 1, 0:512], in_=gsrc2), writes=[*self.ar(1)], dma_key=("c", "ar1"))
                g.op("dve", lambda e, pg=pg: e.tensor_tensor(out=self.arena[:, 2, 0:512], in0=self.ps[:, pg, :], in1=self.arena[:, 0, 0:512], op=ALU.add),
                     reads=[("ps", pg), *self.ar(0)], writes=[*self.ar(2)])
                gidx = 2 * L + sub
                g.op("dve", lambda e, gidx=gidx, half=half: e.tensor_tensor(out=self.G[:, gidx, half * 512:(half + 1) * 512], in0=self.arena[:, 2, 0:512],
                                                                           in1=self.arena[:, 1, 0:512], op=ALU.mult),
                     reads=[*self.ar(2), *self.ar(1)], writes=[("G", gidx, half)])

    def rstd_chain(self, ss_c, ms_c, r_c, n, keys_in, key_out):
        g = self.g
        g.op("dve", lambda e: e.tensor_scalar(out=self.st(ms_c, ms_c + n), in0=self.st(ss_c, ss_c + n), scalar1=1.0 / D, scalar2=EPS,
                                              op0=ALU.mult, op1=ALU.add), reads=keys_in, writes=[("ms", ms_c)])
        g.op("pool", lambda e: e.tensor_tensor(out=self.st(r_c, r_c + n), in0=self.st(ms_c, ms_c + n), in1=self.st(72, 72 + n), op=ALU.pow),
             reads=[("ms", ms_c), "mhalf"], writes=[key_out])

    def prenorm(self, L, sub):
        g = self.g
        self.ensure_mod(L, sub)
        g.phase = self.tag + ".prenorm"
        for b in range(NB):
            sl = self.xs(b)
            g.op("act", lambda e, b=b, sl=sl: e.activation(out=self.junk[:], in_=self.x_sb[:, sl, :], func=AF.Square, accum_out=self.st(b)),
                 reads=[("x", sl)], writes=[("ss", b)] + [("junk", 0), ("junk", 1)])
            if b % 2 == 1:
                self.rstd_chain(b - 1, 4 + b - 1, 8 + b - 1, 2, [("ss", b - 1), ("ss", b)], ("rstd", b // 2))
        for b in range(NB):
            xb = b % 2
            sl = self.xs(b)
            g.op("act", lambda e, b=b, xb=xb, sl=sl: e.activation(out=self.xn[:, xb, :], in_=self.x_sb[:, sl, :], func=AF.Copy, scale=self.st(8 + b)),
                 reads=[("x", sl), ("rstd", b // 2)], writes=[("xn", xb)])
            tb = self.tbank()

            def tr(e, xb=xb, tb=tb):
                ins = None
                for kc in range(KC):
                    ins = e.transpose(out=self.psb(tb)[:, kc * 128:(kc + 1) * 128], in_=self.xn[:, xb, kc * 128:(kc + 1) * 128],
                                      identity=self.ident[:])
                return ins
            g.op("pe", tr, reads=[("xn", xb), "ident"], writes=[("ps", tb)])
            for kc in range(KC):
                def ev(e, kc=kc, b=b, tb=tb):
                    return e.tensor_scalar(out=self.hT[:, kc, b * 128:(b + 1) * 128], in0=self.psb(tb)[:, kc * 128:(kc + 1) * 128],
                                           scalar1=self.AB[:, L, 2 * sub, kc:kc + 1], scalar2=self.AB[:, L, 2 * sub + 1, kc:kc + 1],
                                           op0=ALU.mult, op1=ALU.add)
                g.op("dve", ev, reads=[("ps", tb), ("AB", L, 2 * sub), ("AB", L, 2 * sub + 1)], writes=[("hT", kc, b)])

    def act_keys(self, js, blocks=range(NB)):
        return [("act", j, b) for j in js for b in blocks]

    def hT_keys(self, blocks=range(NB)):
        return [("hT", kc, b) for kc in range(KC) for b in blocks]

    def post_block(self, b, bank0, bank1, gidx):
        g = self.g
        g.op("act", lambda e: e.activation(out=self.junk[:, 0:512], in_=self.ps[:, bank0, :], func=AF.Square, accum_out=self.st(12 + b)),
             reads=[("ps", bank0)], writes=[("ssa", b), ("junk", 0)])
        g.op("act", lambda e: e.activation(out=self.junk[:, 512:1024], in_=self.ps[:, bank1, :], func=AF.Square, accum_out=self.st(16 + b)),
             reads=[("ps", bank1)], writes=[("ssb", b), ("junk", 1)])
        g.op("dve", lambda e: e.tensor_tensor(out=self.st(20 + b), in0=self.st(12 + b), in1=self.st(16 + b), op=ALU.add),
             reads=[("ssa", b), ("ssb", b)], writes=[("ss2", b)])
        g.op("dve", lambda e: e.tensor_scalar(out=self.st(24 + b), in0=self.st(20 + b), scalar1=1.0 / D, scalar2=EPS, op0=ALU.mult, op1=ALU.add),
             reads=[("ss2", b)], writes=[("ms2", b)])
        g.op("pool", lambda e: e.tensor_tensor(out=self.st(28 + b), in0=self.st(24 + b), in1=self.st(72), op=ALU.pow),
             reads=[("ms2", b), "mhalf"], writes=[("rstd2", b)])
        for n, bank in enumerate((bank0, bank1)):
            g.op("dve", lambda e, n=n, bank=bank: e.tensor_tensor(out=self.tres[:, n * 512:(n + 1) * 512], in0=self.ps[:, bank, :],
                                                                 in1=self.G[:, gidx, n * 512:(n + 1) * 512], op=ALU.mult),
                 reads=[("ps", bank), ("G", gidx, n)], writes=[("tres", n)])
        sl = self.xs(b)
        g.op("dve", lambda e: e.scalar_tensor_tensor(out=self.x_sb[:, sl, :], in0=self.tres[:], scalar=self.st(28 + b), in1=self.x_sb[:, sl, :],
                                                     op0=ALU.mult, op1=ALU.add),
             reads=[("tres", 0), ("tres", 1), ("rstd2", b), ("x", sl)], writes=[("x", sl)])

    def matmul2(self, specs, in_chunk, in_keys, gidx, on_done=None, nxt=None):
        g = self.g
        if nxt is not None:
            self.ensure_mod(*nxt)
        g.phase = self.tag + ".mm2a"
        held = [self.mbank() for _ in range(NB)]
        last = len(specs[0]) - 1
        for gi, (name, parts, j0, nj) in enumerate(specs[0]):
            s, keys = self.get_unit(name, parts, nk=nj)
            for b in range(NB):
                def mm(e, s=s, b=b, j0=j0, nj=nj, gi=gi):
                    ins = None
                    for jj in range(nj):
                        ins = e.matmul(self.ps[:, held[b], :], lhsT=in_chunk(j0 + jj, b), rhs=self.wring[:, s, jj, :],
                                       start=(gi == 0 and jj == 0), stop=(gi == last and jj == nj - 1))
                    return ins
                g.op("pe", mm, reads=keys + in_keys(b), writes=[("ps", held[b])])
        units = []
        for (name, parts, j0, nj) in specs[1]:
            s, keys = self.get_unit(name, parts, nk=nj)
            units.append((s, keys, j0, nj))
        tot = sum(u[3] for u in units)
        g.phase = self.tag + ".mm2b"
        for b in range(NB):
            bank = self.mbank()

            def mm1(e, b=b, bank=bank):
                ins = None
                i = 0
                for (s, _, j0, nj) in units:
                    for jj in range(nj):
                        ins = e.matmul(self.ps[:, bank, :], lhsT=in_chunk(j0 + jj, b), rhs=self.wring[:, s, jj, :],
                                       start=(i == 0), stop=(i == tot - 1))
                        i += 1
                return ins
            g.op("pe", mm1, reads=[k for u in units for k in u[1]] + in_keys(b), writes=[("ps", bank)])
            self.post_block(b, held[b], bank, gidx)
            if on_done is not None:
                on_done(b)

    def conv_pre(self, L):
        g = self.g
        r0 = 0
        g.op("pool", lambda e: e.memset(self.tres[:], 0.0), writes=[("tres", 0), ("tres", 1)])
        g.op("sp", lambda e: e.dma_start(out=self.tres[0:2, :], in_=self.xa[r0:r0 + 2, :]), reads=[("tres", 0), ("tres", 1)],
             writes=[("tres", 0), ("tres", 1)], dma_key=("c", "xpre"))
        g.op("act", lambda e: e.activation(out=self.junk[:], in_=self.tres[:], func=AF.Square, accum_out=self.st(32)),
             reads=[("tres", 0), ("tres", 1)], writes=["sspre"] + [("junk", 0), ("junk", 1)])
        self.rstd_chain(32, 33, 34, 1, ["sspre"], "rstdpre")
        g.op("act", lambda e: e.activation(out=self.xn[:, 1, :], in_=self.tres[:], func=AF.Copy, scale=self.st(34)),
             reads=[("tres", 0), ("tres", 1), "rstdpre"], writes=[("xn", 1)])
        tb = self.tbank()

        def tr(e):
            ins = None
            for kc in range(KC):
                ins = e.transpose(out=self.psb(tb)[:, kc * 128:(kc + 1) * 128], in_=self.xn[:, 1, kc * 128:(kc + 1) * 128], identity=self.ident[:])
            return ins
        g.op("pe", tr, reads=[("xn", 1), "ident"], writes=[("ps", tb)])
        for kc in range(KC):
            g.op("dve", lambda e, kc=kc: e.tensor_scalar(out=self.hpre[:, kc, :], in0=self.psb(tb)[:, kc * 128:kc * 128 + 2],
                                                        scalar1=self.AB[:, L, 0, kc:kc + 1], scalar2=self.AB[:, L, 1, kc:kc + 1],
                                                        op0=ALU.mult, op1=ALU.add),
                 reads=[("ps", tb), ("AB", L, 0), ("AB", L, 1)], writes=[("hpre", kc)])

    def conv_mixer(self, first, mask_carry=False):
        g = self.g
        L = 0
        self.tag = "conv"
        self.ensure_mod(L, 0)
        g.phase = "conv.pre"
        if first:
            self.conv_pre(L)
        if mask_carry:
            ck = [("carry", fc) for fc in range(KC)]
            g.op("dve", lambda e: e.tensor_scalar(out=self.carry[:].rearrange("p k t -> p (k t)"), in0=self.carry[:].rearrange("p k t -> p (k t)"),
                                                  scalar1=self.cmask_sb[:, 0:1], scalar2=None, op0=ALU.mult),
                 reads=ck + ["cmask"], writes=ck)
        self.prenorm(L, 0)
        g.phase = "conv.mm1"
        hk = self.hT_keys()
        for fc in range(KC):
            parts = [(t * 128, 128, self.wsrc(self.conv_w_in, t * D + fc * 128, 128)) for t in range(3)]
            s, keys = self.get_unit(("cin", fc), parts)
            banks = [self.mbank() for _ in range(3)]
            halves = ((0, NT // 2), (NT // 2, NT)) if fc == 0 else ((0, NT),)
            for (c0, c1) in halves:
                for t in range(3):
                    def mm(e, t=t, s=s, bank=banks[t], c0=c0, c1=c1):
                        ins = None
                        for kc in range(KC):
                            ins = e.matmul(self.ps[:, bank, c0:c1], lhsT=self.wring[:, s, kc, t * 128:(t + 1) * 128], rhs=self.hT[:, kc, c0:c1],
                                           start=(kc == 0), stop=(kc == KC - 1))
                        return ins
                    g.op("pe", mm, reads=keys + self.hT_keys(range(c0 // 128, c1 // 128)), writes=[("ps", banks[t])])
            ub = 1 + (fc % 2)
            u = self.arena[:, ub, 0:514]
            if first:
                pp = self.mbank()

                def mmp(e, s=s, pp=pp):
                    ins = None
                    for ti, t in enumerate((1, 2)):
                        for kc in range(KC):
                            ins = e.matmul(self.ps[:, pp, 2 * ti:2 * ti + 2], lhsT=self.wring[:, s, kc, t * 128:(t + 1) * 128], rhs=self.hpre[:, kc, :],
                                           start=(kc == 0), stop=(kc == KC - 1))
                    return ins
                g.op("pe", mmp, reads=keys + [("hpre", kc) for kc in range(KC)], writes=[("ps", pp)])
                g.op("act", lambda e, pp=pp: e.copy(out=self.vpre[:], in_=self.ps[:, pp, 2:4]), reads=[("ps", pp)], writes=["vpre"])
                g.op("dve", lambda e, pp=pp, fc=fc: e.scalar_tensor_tensor(out=self.carry[:, fc, :], in0=self.ps[:, pp, 0:2], scalar=self.cmask_sb[:, 0:1],
                                                                          in1=self.vpre[:], op0=ALU.mult, op1=ALU.mult),
                     reads=[("ps", pp), "vpre", "cmask"], writes=[("carry", fc)])
            g.op("act", lambda e, bank=banks[2]: e.copy(out=self.arena[:, 0, 0:512], in_=self.ps[:, bank, :]), reads=[("ps", banks[2])], writes=[*self.ar(0)])
            g.op("dve", lambda e, u=u, fc=fc: e.tensor_copy(out=u[:, 0:2], in_=self.carry[:, fc, :]), reads=[("carry", fc)], writes=[("ar", ub, 0)])
            g.op("dve", lambda e, u=u, bank=banks[1]: e.tensor_tensor(out=u[:, 2:514], in0=self.ps[:, bank, :], in1=self.arena[:, 0, 0:512], op=ALU.mult),
                 reads=[("ps", banks[1]), *self.ar(0)], writes=[*self.ar(ub)])
            g.op("dve", lambda e, u=u, fc=fc: e.tensor_copy(out=self.carry[:, fc, :], in_=u[:, 512:514]), reads=[*self.ar(ub)], writes=[("carry", fc)])
            t1 = self.arena[:, 3, 0:512]
            t2 = self.arena[:, 4, 0:512]
            g.op("act", lambda e, u=u, fc=fc: e.activation(out=t1, in_=u[:, 2:514], func=AF.Copy, scale=self.cw[:, fc, 2:3]),
                 reads=[*self.ar(ub), "cw"], writes=[*self.ar(3)])
            g.op("dve", lambda e, u=u, fc=fc: e.scalar_tensor_tensor(out=t2, in0=u[:, 1:513], scalar=self.cw[:, fc, 1:2], in1=t1, op0=ALU.mult, op1=ALU.add),
                 reads=[*self.ar(ub), *self.ar(3), "cw"], writes=[*self.ar(4)])
            g.op("dve", lambda e, u=u, fc=fc: e.scalar_tensor_tensor(out=t1, in0=u[:, 0:512], scalar=self.cw[:, fc, 0:1], in1=t2, op0=ALU.mult, op1=ALU.add),
                 reads=[*self.ar(ub), *self.ar(4), "cw"], writes=[*self.ar(3)])
            g.op("dve", lambda e, fc=fc, bank=banks[0]: e.tensor_tensor(out=self.act[:, fc, :], in0=self.ps[:, bank, :], in1=t1, op=ALU.mult),
                 reads=[("ps", banks[0]), *self.ar(3)], writes=self.act_keys([fc]))
        specs = [[(("cout", n), [(0, 512, self.wsrc(self.conv_w_out, n * 512, 512))], 0, KC)] for n in range(2)]
        self.matmul2(specs, lambda j, b: self.act[:, j, b * 128:(b + 1) * 128], lambda b: self.act_keys(range(KC), [b]), 0, nxt=(0, 1))

    def ffn(self, L, on_done=None):
        g = self.g
        self.tag = "ffn%d" % L
        self.prenorm(L, 1)
        g.phase = self.tag + ".mm1"
        hk = self.hT_keys()
        gu = self.ffn_gu[L]
        for jp in range(JC // 2):
            j0 = 2 * jp
            parts = [(0, 256, self.wsrc(gu, j0 * 128, 256)), (256, 256, self.wsrc(gu, DFF + j0 * 128, 256))]
            s, keys = self.get_unit(("gu", L, jp), parts)
            for jj in range(2):
                j = j0 + jj
                pg, pu = self.mbank(), self.mbank()
                halves = ((0, NT // 2), (NT // 2, NT)) if j == 0 else ((0, NT),)
                for (t0, t1) in halves:
                    for which, bank in ((0, pg), (1, pu)):
                        def mm(e, s=s, c0=which * 256 + jj * 128, bank=bank, t0=t0, t1=t1):
                            ins = None
                            for kc in range(KC):
                                ins = e.matmul(self.ps[:, bank, t0:t1], lhsT=self.wring[:, s, kc, c0:c0 + 128], rhs=self.hT[:, kc, t0:t1],
                                               start=(kc == 0), stop=(kc == KC - 1))
                            return ins
                        g.op("pe", mm, reads=keys + self.hT_keys(range(t0 // 128, t1 // 128)), writes=[("ps", bank)])
                ab = j % 3
                g.op("act", lambda e, pg=pg, ab=ab: e.activation(out=self.arena[:, ab, 0:512], in_=self.ps[:, pg, :], func=AF.Silu),
                     reads=[("ps", pg)], writes=[*self.ar(ab)])
                g.op("dve", lambda e, pu=pu, ab=ab, j=j: e.tensor_tensor(out=self.act[:, j, :], in0=self.ps[:, pu, :], in1=self.arena[:, ab, 0:512], op=ALU.mult),
                     reads=[("ps", pu), *self.ar(ab)], writes=self.act_keys([j]))
        wd = self.ffn_down[L]
        specs = [[(("down", L, n, gi), [(0, 512, self.wsrc(wd, n * 512, 512, r0=j0, nk=nj))], j0, nj)
                  for gi, (j0, nj) in enumerate(((0, 8), (8, 8), (16, 6)))] for n in range(2)]
        self.matmul2(specs, lambda j, b: self.act[:, j, b * 128:(b + 1) * 128], lambda b: self.act_keys(range(JC), [b]), 2 * L + 1, on_done=on_done, nxt=((1, 0) if (L == 0 and self.do_l1) else None))

    def attn_kv(self, kv_only):
        g = self.g
        g.phase = "attn.qkv"
        hk = self.hT_keys()
        W = self.attn_w_qkv
        g.op("dve", lambda e: e.tensor_copy(out=self.KT[:, :, 0:NT], in_=self.KT[:, :, NT:2 * NT]),
             reads=[("KT", fc, 1) for fc in range(KC)], writes=[("KT", fc, 0) for fc in range(KC)])
        g.op("dve", lambda e: e.tensor_copy(out=self.V[:, 0:NB, :], in_=self.V[:, NB:2 * NB, :]),
             reads=[("V", NB + b, n) for b in range(NB) for n in range(2)], writes=[("V", b, n) for b in range(NB) for n in range(2)])
        for n in range(2):
            s, keys = self.get_unit(("v", n), [(0, 512, self.wsrc(W, 2 * D + n * 512, 512))])
            for b in range(NB):
                bank = self.mbank()

                def mmv(e, s=s, b=b, bank=bank):
                    ins = None
                    for kc in range(KC):
                        ins = e.matmul(self.ps[:, bank, :], lhsT=self.hT[:, kc, b * 128:(b + 1) * 128], rhs=self.wring[:, s, kc, :],
                                       start=(kc == 0), stop=(kc == KC - 1))
                    return ins
                g.op("pe", mmv, reads=keys + self.hT_keys([b]), writes=[("ps", bank)])
                g.op("dve", lambda e, b=b, n=n, bank=bank: e.tensor_copy(out=self.V[:, NB + b, n * 512:(n + 1) * 512], in_=self.ps[:, bank, :]),
                     reads=[("ps", bank)], writes=[("V", NB + b, n)])
        for which in ((1,) if kv_only else (1, 0)):
            for u in range(2):
                s, keys = self.get_unit(("qk", which, u), [(0, 512, self.wsrc(W, which * D + u * 512, 512))])
                for fcl in range(4):
                    fc = u * 4 + fcl
                    bank = self.mbank()

                    def mm(e, s=s, fcl=fcl, bank=bank):
                        ins = None
                        for kc in range(KC):
                            ins = e.matmul(self.ps[:, bank, :], lhsT=self.wring[:, s, kc, fcl * 128:(fcl + 1) * 128], rhs=self.hT[:, kc, :],
                                           start=(kc == 0), stop=(kc == KC - 1))
                        return ins
                    g.op("pe", mm, reads=keys + hk, writes=[("ps", bank)])
                    if which == 1:
                        g.op("act", lambda e, fc=fc, bank=bank: e.copy(out=self.KT[:, fc, NT:2 * NT], in_=self.ps[:, bank, :]),
                             reads=[("ps", bank)], writes=[("KT", fc, 1)])
                    else:
                        g.op("act", lambda e, fc=fc, bank=bank: e.activation(out=self.act[:, 8 + fc, :], in_=self.ps[:, bank, :], func=AF.Copy, scale=0.125),
                             reads=[("ps", bank)], writes=self.act_keys([8 + fc]))

    def attention(self, mask_halo):
        g = self.g
        g.phase = "attn.heads"
        LM, LT, LP = 1, 3, 5
        n_items = NB * NH
        pvb = 2

        def rs_col(qb, h):
            return 40 + 16 * (qb % 2) + h

        def stage_s(i):
            qb, h = divmod(i, NH)
            q0 = qb * 128
            fc = h // 2
            sp_ = 4 + 2 * (i % 2)
            nh = NT - q0 if mask_halo else 0

            def mm(e):
                qT = self.act[:, 8 + fc, q0:q0 + 128] if h % 2 == 0 else self.hT[:, fc, q0:q0 + 128]
                e.matmul(self.ps[:, sp_, :], lhsT=qT, rhs=self.KT[:, fc, q0:q0 + 512], start=True, stop=(nh == 0))
                if nh:
                    e.matmul(self.ps[:, sp_, 0:nh], lhsT=self.ones2[:], rhs=self.mrow[:, 0:nh], start=False, stop=True)
                return e.matmul(self.ps[:, sp_ + 1, 0:128], lhsT=qT, rhs=self.KT[:, fc, q0 + 512:q0 + 640], start=True, stop=True)
            rk = [("act", 8 + fc, qb) if h % 2 == 0 else ("hT", fc, qb), ("KT", fc, 0), ("KT", fc, 1)] + (["ones2", "mrow0"] if nh else [])
            g.op("pe", mm, reads=rk, writes=[("ps", sp_), ("ps", sp_ + 1)])
            tS, tk = self.tsbuf(i % 3)
            sview = self.ps[:, sp_:sp_ + 2, :].rearrange("p a b -> p (a b)")[:, 0:640]
            g.op("dve", lambda e: e.tensor_tensor(out=tS, in0=sview, in1=self.bias[:, h, :], op=ALU.add),
                 reads=[("ps", sp_), ("ps", sp_ + 1), "bias"], writes=tk)

        def stage_m(i):
            qb, h = divmod(i, NH)
            tb = i % 3
            tS, tk = self.tsbuf(tb)
            g.op("dve", lambda e: e.reduce_max(out=self.st(36 + tb), in_=tS, axis=AX.X, negate=True), reads=tk, writes=[("nm", tb)])
            p = self.pbuf(tb)
            g.op("act", lambda e: e.activation(out=p, in_=tS, func=AF.Exp, bias=self.st(36 + tb), scale=1.0, accum_out=self.st(rs_col(qb, h))),
                 reads=tk + [("nm", tb)], writes=[self.pkey(tb), ("rs", qb % 2, h)])

        def stage_t(i):
            tb = i % 3
            p = self.pbuf(tb)
            bank = self.tbank()

            def tr(e):
                ins = None
                for j in range(5):
                    ins = e.transpose(out=self.psb(bank)[:, j * 128:(j + 1) * 128], in_=p[:, j * 128:(j + 1) * 128], identity=self.ident[:])
                return ins
            g.op("pe", tr, reads=[self.pkey(tb), "ident"], writes=[("ps", bank)])
            pT = self.ptbuf(tb)
            g.op("act", lambda e: e.copy(out=pT, in_=self.psb(bank)[:, 0:640]), reads=[("ps", bank)], writes=[self.ptkey(tb)])

        def stage_pv(i):
            qb, h = divmod(i, NH)
            tb = i % 3
            pT = self.ptbuf(tb)
            bank = pvb + h // 8
            c0 = (h % 8) * 64

            def mm(e):
                ins = None
                for j in range(5):
                    ins = e.matmul(self.ps[:, bank, c0:c0 + 64], lhsT=pT[:, j * 128:(j + 1) * 128], rhs=self.V[:, qb + j, h * 64:(h + 1) * 64],
                                   start=(j == 0), stop=(j == 4))
                return ins
            g.op("pe", mm, reads=[self.ptkey(tb)] + [("V", qb + j, h // 8) for j in range(5)], writes=[("ps", bank)])
            if h == NH - 1:
                epilogue_a(qb)
                pending.append((i + LP + 2, qb))

        pending = []

        def epilogue_a(qb):
            c = 40 + 16 * (qb % 2)
            g.op("dve", lambda e: e.reciprocal(out=self.st(76, 92), in_=self.st(c, c + 16)), reads=[("rs", qb % 2, h) for h in range(NH)], writes=["rr"])
            pvv = self.ps[:, pvb:pvb + 2, :].rearrange("p a (h d) -> p (a h) d", d=64)
            g.op("dve", lambda e: e.tensor_tensor(out=self.attn_sb[:].rearrange("p (h d) -> p h d", d=64), in0=pvv,
                                                  in1=self.st(76, 92).unsqueeze(2).to_broadcast([128, NH, 64]), op=ALU.mult),
                 reads=[("ps", pvb), ("ps", pvb + 1), "rr"], writes=["attn_sb"])

        def epilogue_b(qb):
            q0 = qb * 128
            bank = self.tbank()

            def tra(e):
                ins = None
                for fc in range(KC):
                    ins = e.transpose(out=self.psb(bank)[:, fc * 128:(fc + 1) * 128], in_=self.attn_sb[:, fc * 128:(fc + 1) * 128], identity=self.ident[:])
                return ins
            g.op("pe", tra, reads=["attn_sb", "ident"], writes=[("ps", bank)])
            g.op("act", lambda e: e.copy(out=self.act[:, 0:KC, q0:q0 + 128], in_=self.psb(bank).rearrange("p (k t) -> p k t", k=KC)),
                 reads=[("ps", bank)], writes=self.act_keys(range(KC), [qb]))

        for step in range(n_items + LP):
            if step < n_items:
                stage_s(step)
            if LM <= step < n_items + LM:
                stage_m(step - LM)
            if LT <= step < n_items + LT:
                stage_t(step - LT)
            if LP <= step:
                stage_pv(step - LP)
            while pending and pending[0][0] <= step:
                epilogue_b(pending.pop(0)[1])
        while pending:
            epilogue_b(pending.pop(0)[1])

    def attn_mixer(self, kv_only, mask_halo):
        L = 1
        self.tag = "attn"
        self.prenorm(L, 0)
        self.attn_kv(kv_only)
        if kv_only:
            return
        g = self.g
        g.phase = "attn.qmask"
        for fc in range(KC):
            qk = self.act_keys([8 + fc])
            hk = [("hT", fc, b) for b in range(NB)]
            g.op("dve", lambda e, fc=fc: e.tensor_copy(out=self.hT[64:128, fc, :], in_=self.act[64:128, 8 + fc, :]), reads=qk + hk, writes=hk)
            g.op("pool", lambda e, fc=fc: e.memset(self.hT[0:64, fc, :], 0.0), reads=hk, writes=hk)
            g.op("pool", lambda e, fc=fc: e.memset(self.act[64:128, 8 + fc, :], 0.0), reads=qk, writes=qk)
        self.attention(mask_halo)
        specs = [[(("aout", n), [(0, 512, self.wsrc(self.attn_w_out, n * 512, 512))], 0, KC)] for n in range(2)]
        self.matmul2(specs, lambda j, b: self.act[:, j, b * 128:(b + 1) * 128], lambda b: self.act_keys(range(KC), [b]), 2, nxt=(1, 1))

    def build(self):
        g = self.g
        self.prologue()
        if self.do_l1:
            g.op("pool", lambda e: e.memset(self.KT[:], 0.0), writes=[("KT", fc, h) for fc in range(KC) for h in range(2)])
            g.op("pool", lambda e: e.memset(self.V[:], 0.0), writes=[("V", b, n) for b in range(2 * NB) for n in range(2)])
        xoff = 2 if self.do_l0 else 0

        def load_block(t, b):
            r = xoff + t * NT + b * 128
            sl = self.xs(b, t)
            g.op("sp", lambda e: e.dma_start(out=self.x_sb[:, sl, :], in_=self.xa[r:r + 128, :]), writes=[("x", sl)], dma_key=("xld", sl))

        def store_block(t, b):
            r = (t - self.first_out_tile) * NT + b * 128
            sl = self.xs(b, t)
            g.op("sp", lambda e: e.dma_start(out=self.out[r:r + 128, :], in_=self.x_sb[:, sl, :]), reads=[("x", sl)], writes=[("out", b)],
                 dma_key=("xst", sl))

        def tile_tail(t):
            def cb(b):
                g.phase = "io"
                if t >= self.first_out_tile:
                    store_block(t, b)
                if t + 1 < self.n_tiles and b + 1 < NB:
                    load_block(t + 1, b + 1)
            return cb

        for b in range(NB):
            load_block(0, b)
        for t in range(self.n_tiles):
            self.tile_t = t
            if t + 1 < self.n_tiles:
                load_block(t + 1, 0)
            kv_only = self.do_l1 and t == 0
            if self.do_l0:
                self.conv_mixer(first=(t == 0), mask_carry=(self.mode == "fused" and t == self.first_out_tile))
                self.ffn(0, on_done=None if self.do_l1 else tile_tail(t))
            if self.do_l1:
                self.attn_mixer(kv_only, mask_halo=(t == 1))
                if kv_only:
                    for b in range(NB):
                        tile_tail(t)(b)
                else:
                    self.ffn(1, on_done=tile_tail(t))
        g.op("sp", lambda e: None, reads=[("out", b) for b in range(NB)])
        g.finalize(self.ctx)
        self.ctx.close()
        return self.nc


def _bias_tile(rel_bias):
    qi = np.arange(128)[:, None]
    kj = np.arange(640)[None, :]
    dist = qi + 512 - kj
    idx = np.clip(dist, -256, 256) + 256
    cq, ck = qi // 64, kj // 64
    vis = (ck >= cq) & (ck <= cq + 8)
    gathered = rel_bias[:, idx]
    full = np.where(vis[None], gathered, np.float32(NEG)).astype(np.float32)
    return np.ascontiguousarray(full.transpose(1, 0, 2))


def _common_inputs(c_b, ada_w, ada_b, norm_gains, conv_w_in, conv_w, conv_w_out, attn_w_qkv, attn_rel_bias, attn_w_out,
                   ffn_w_gate_up, ffn_w_down, seq_start):
    f = np.float32
    hm = np.zeros((2, 512), dtype=ml_dtypes.bfloat16)
    if seq_start:
        hm[:] = NEG
    return {
        "cvec": np.ascontiguousarray(c_b.reshape(KC, 128).T).astype(f),
        "ada_w": ada_w, "ada_b": ada_b,
        "ada_bT": np.ascontiguousarray(ada_b.reshape(2, 48, 128).transpose(0, 2, 1)),
        "gains": norm_gains,
        "gainsT": np.ascontiguousarray(norm_gains.reshape(2, 4, KC, 128).transpose(0, 1, 3, 2)),
        "conv_w_in": conv_w_in[0],
        "conv_wT": np.ascontiguousarray(conv_w[0].reshape(3, KC, 128).transpose(2, 1, 0)),
        "conv_w_out": conv_w_out[0],
        "attn_w_qkv": attn_w_qkv[0], "attn_w_out": attn_w_out[0],
        "ffn_gu": ffn_w_gate_up, "ffn_down": ffn_w_down,
        "bias_t": _bias_tile(attn_rel_bias[0]),
        "hmask": hm,
        "cmask": np.full((128, 1), 0.0 if seq_start else 1.0, dtype=f),
        "idbf": np.eye(128).astype(ml_dtypes.bfloat16),
    }


_NC_CACHE = {}


def _program(mode):
    if mode not in _NC_CACHE:
        b = Builder(mode)
        b.declare()
        _NC_CACHE[mode] = b.build()
    return _NC_CACHE[mode]


def _rows(xb, lo, hi):
    out = np.zeros((hi - lo, xb.shape[1]), dtype=np.float32)
    a = max(lo, 0)
    if hi > a:
        out[a - lo:] = xb[a:hi]
    return out


def kernel(x, c, ada_w, ada_b, norm_gains, conv_w_in, conv_w, conv_w_out, attn_w_qkv, attn_rel_bias, attn_w_out,
           ffn_w_gate_up, ffn_w_down):
    args = [np.asarray(a, dtype=np.float32) for a in (ada_w, ada_b, norm_gains, conv_w_in, conv_w, conv_w_out, attn_w_qkv,
                                                     attn_rel_bias, attn_w_out, ffn_w_gate_up, ffn_w_down)]
    x = np.asarray(x, dtype=np.float32)
    c = np.asarray(c, dtype=np.float32)
    n_cores = 8
    in_maps = []
    for k in range(n_cores):
        b, s0 = k // 4, (k % 4) * TOK_CORE
        m = _common_inputs(c[b], *args, seq_start=(s0 == 0))
        m["xa"] = _rows(x[b], s0 - HALO - 2, s0 + TOK_CORE)
        in_maps.append(m)
    res = run_bass_kernel_spmd(_program("fused"), in_maps, core_ids=list(range(n_cores)))
    out = np.empty_like(x)
    for k in range(n_cores):
        b, s0 = k // 4, (k % 4) * TOK_CORE
        out[b, s0:s0 + TOK_CORE] = res.results[k]["out"]
    return out
```
